# Optimizing a Trainium2 kernel written in Bass

```python
import jax, jax.numpy as jnp
from jax import lax
import numpy as np

D_MODEL = 1024
BATCH = 4
SEQ = 4096
DEPTH = 2

CHUNK = 64
CONV_WIDTH = 3
D_CONV = 1024
D_GMLP = 1024
GMLP_BLOCK = 128
N_GROUPS_GMLP = 8
D_POOL = 1024
POOL_WINDOWS = (2, 4, 8, 16)
POOL_GROUP = D_POOL // len(POOL_WINDOWS)
N_BRANCHES = 3
D_FF = 2816
D_IN = 3 * D_CONV + 2 * D_GMLP + D_POOL + N_BRANCHES * D_MODEL
ALPHA = (2 * DEPTH) ** 0.25
BETA = (8 * DEPTH) ** -0.25
LN_EPS = 1e-5

kernel_name = "hybrid_conv_gmlp_pool_deepnorm_adaln"


def layer_norm(x, g, b):
    xf = x.astype(jnp.float32)
    mu = jnp.mean(xf, axis=-1, keepdims=True)
    var = jnp.mean(jnp.square(xf - mu), axis=-1, keepdims=True)
    y = (xf - mu) * lax.rsqrt(var + LN_EPS)
    return (y * g.astype(jnp.float32) + b.astype(jnp.float32)).astype(x.dtype)


def causal_dwconv(x, w):
    k, ch = w.shape
    return lax.conv_general_dilated(
        x, w[:, None, :].astype(x.dtype), window_strides=(1,), padding=[(k - 1, 0)],
        dimension_numbers=("NWC", "WIO", "NWC"), feature_group_count=ch)


def spatial_gating(u, v, ln_g, ln_b, w_s, b_s):
    bn, s, _ = v.shape
    v = layer_norm(v, ln_g, ln_b)
    vb = v.reshape(bn, s // GMLP_BLOCK, GMLP_BLOCK, N_GROUPS_GMLP, D_GMLP // N_GROUPS_GMLP)
    pos = jnp.arange(GMLP_BLOCK)
    allowed = (pos[None, :] // CHUNK) <= (pos[:, None] // CHUNK)
    w = jnp.where(allowed[None], w_s, jnp.zeros_like(w_s))
    mixed = jnp.einsum("gij,bnjgc->bnigc", w, vb) + b_s.T[None, None, :, :, None]
    return u * mixed.reshape(bn, s, D_GMLP)


def multiscale_pool(p, w_pool, scale):
    s = p.shape[1]
    pf = p.astype(jnp.float32)
    cs = jnp.cumsum(pf, axis=1)
    t = jnp.arange(1, s + 1, dtype=jnp.float32)
    outs = []
    for k, win in enumerate(POOL_WINDOWS):
        lo, hi = k * POOL_GROUP, (k + 1) * POOL_GROUP
        csk = cs[..., lo:hi]
        prev = jnp.pad(csk, ((0, 0), (win, 0), (0, 0)))[:, :s]
        mean = (csk - prev) / jnp.minimum(t, float(win))[None, :, None]
        d = (mean - pf[..., lo:hi]).astype(p.dtype)
        outs.append(d @ w_pool[k])
    return jnp.concatenate(outs, axis=-1) * scale


def setup_inputs(seed: int = 0) -> dict:
    key = jax.random.key(seed)
    ks = jax.random.split(key, 26)
    f32 = jnp.float32

    def nrm(k, shape, s):
        return jax.random.normal(k, shape, f32) * s

    L = DEPTH
    w_in_scale = D_MODEL ** -0.5
    return {
        "x": nrm(ks[0], (BATCH, SEQ, D_MODEL), 1.0),
        "c": nrm(ks[1], (BATCH, D_MODEL), 1.0),
        "w_ada": nrm(ks[2], (L, D_MODEL, 6 * D_MODEL), 0.5 * D_MODEL ** -0.5),
        "b_ada": nrm(ks[3], (L, 6 * D_MODEL), 0.02),
        "w_in": nrm(ks[4], (L, D_MODEL, D_IN), w_in_scale),
        "b_in": nrm(ks[5], (L, D_IN), 0.01),
        "conv_a": nrm(ks[6], (L, CONV_WIDTH, D_CONV), 0.5),
        "w_a_out": nrm(ks[7], (L, D_CONV, D_MODEL), D_CONV ** -0.5),
        "ln_v_g": 1.0 + nrm(ks[8], (L, D_GMLP), 0.02),
        "ln_v_b": nrm(ks[9], (L, D_GMLP), 0.02),
        "w_spatial": nrm(ks[10], (L, N_GROUPS_GMLP, GMLP_BLOCK, GMLP_BLOCK), 0.5 * GMLP_BLOCK ** -0.5),
        "b_spatial": 1.0 + nrm(ks[11], (L, N_GROUPS_GMLP, GMLP_BLOCK), 0.02),
        "w_b_out": nrm(ks[12], (L, D_GMLP, D_MODEL), D_GMLP ** -0.5),
        "w_pool": nrm(ks[13], (L, len(POOL_WINDOWS), POOL_GROUP, POOL_GROUP), POOL_GROUP ** -0.5),
        "pool_scale": 1.0 + nrm(ks[14], (L, D_POOL), 0.02),
        "w_o": nrm(ks[15], (L, D_MODEL, D_MODEL), BETA * D_MODEL ** -0.5),
        "ln1_g": 1.0 + nrm(ks[16], (L, D_MODEL), 0.02),
        "ln1_b": nrm(ks[17], (L, D_MODEL), 0.02),
        "w_up": nrm(ks[18], (L, D_MODEL, 2 * D_FF), w_in_scale),
        "b_up": nrm(ks[19], (L, 2 * D_FF), 0.01),
        "conv_ffn": nrm(ks[20], (L, CONV_WIDTH, D_FF), 0.5),
        "conv_ffn_b": nrm(ks[21], (L, D_FF), 0.01),
        "w_down": nrm(ks[22], (L, D_FF, D_MODEL), BETA * D_FF ** -0.5),
        "ln2_g": 1.0 + nrm(ks[23], (L, D_MODEL), 0.02),
        "ln2_b": nrm(ks[24], (L, D_MODEL), 0.02),
    }


def reference(x, c, w_ada, b_ada, w_in, b_in, conv_a, w_a_out, ln_v_g, ln_v_b,
              w_spatial, b_spatial, w_b_out, w_pool, pool_scale, w_o, ln1_g, ln1_b,
              w_up, b_up, conv_ffn, conv_ffn_b, w_down, ln2_g, ln2_b):
    split_points = [D_CONV, 2 * D_CONV, 3 * D_CONV, 3 * D_CONV + D_GMLP,
                    3 * D_CONV + 2 * D_GMLP, 3 * D_CONV + 2 * D_GMLP + D_POOL]
    c_act = jax.nn.silu(c)
    for l in range(DEPTH):
        ada = (c_act @ w_ada[l] + b_ada[l])[:, None, :]
        sh1, sc1, gt1, sh2, sc2, gt2 = jnp.split(ada, 6, axis=-1)

        h = x * (1.0 + sc1) + sh1
        z = h @ w_in[l] + b_in[l]
        zb, zc, zx, zu, zv, zp, zg = jnp.split(z, split_points, axis=-1)
        y_a = (zb * causal_dwconv(zc * zx, conv_a[l])) @ w_a_out[l]
        y_b = spatial_gating(jax.nn.gelu(zu), jax.nn.gelu(zv), ln_v_g[l], ln_v_b[l],
                             w_spatial[l], b_spatial[l]) @ w_b_out[l]
        y_c = multiscale_pool(zp, w_pool[l], pool_scale[l])
        g_a, g_b, g_c = jnp.split(jax.nn.sigmoid(zg), 3, axis=-1)
        merged = g_a * y_a + g_b * y_b + g_c * y_c
        x = layer_norm(ALPHA * x + gt1 * (merged @ w_o[l]), ln1_g[l], ln1_b[l])

        h = x * (1.0 + sc2) + sh2
        up_a, up_g = jnp.split(h @ w_up[l] + b_up[l], 2, axis=-1)
        f = jax.nn.gelu(causal_dwconv(up_a, conv_ffn[l]) + conv_ffn_b[l]) * up_g
        x = layer_norm(ALPHA * x + gt2 * (f @ w_down[l]), ln2_g[l], ln2_b[l])
    return x
```

```python
import numpy as np
from contextlib import ExitStack
import concourse.bass as bass
import concourse.mybir as mybir
from concourse.bass_utils import run_bass_kernel_spmd

F32 = mybir.dt.float32
BF16 = mybir.dt.bfloat16
AF = mybir.ActivationFunctionType
ALU = mybir.AluOpType
ESZ = {F32: 4, BF16: 2}
PSUM_NAMES = ("PA", "PB", "PC", "PD")

D = 1024
SEQ = 4096
NBATCH = 4
DEPTH = 2
DFF = 2816
NFC = DFF // 128
DIN = 9216
ALPHA = (2 * DEPTH) ** 0.25
EPS = 1e-5
POOL_W = (2, 4, 8, 16)
NBLK_IN = 18
TOK_IN = NBLK_IN * 128
TOK_OUT = 2048
TMAX = 1152
LEAD = 6
EMAX = TMAX + LEAD
NSLOT = 4
SLOT_ELEMS = 4096

V_BIN = 0
V_BADA = 72
V_CONVA = 120
V_LN1G = 144
V_LN1B = 152
V_LN2G = 160
V_LN2B = 168
V_PSC = 176
V_BUP = 184
V_CONVF = 228
V_CONVFB = 294
V_LNVG = 316
V_LNVB = 324
NV = 332


def _rng(ap):
    a = ap.ap
    esz = ESZ[ap.dtype]
    pstep = a[0][0]
    off = ap.offset % pstep if pstep > 0 else ap.offset
    span = 1
    for step, cnt in a[1:]:
        span += (cnt - 1) * abs(step)
    lo, hi = off * esz, (off + span) * esz
    if ap.tensor.name in PSUM_NAMES:
        lo = (lo // 2048) * 2048
        hi = -(-hi // 2048) * 2048
    return ap.tensor.name, lo, hi


class Rec:
    def __init__(self, nc, es):
        self.nc = nc
        self.es = es
        self.engs = ["pe", "act", "dve", "pool", "sp"]
        self.sems = {}
        self.val = {}
        self.waited = {e: {} for e in self.engs}
        self.ops = {e: [] for e in self.engs}
        self.tr = {}
        self.pending = {e: ([], []) for e in self.engs}
        for e in self.engs:
            self.newsem(e)

    def newsem(self, name):
        self.sems[name] = self.es.enter_context(self.nc.semaphore("s_" + name))
        self.val[name] = 0

    def _need(self, reads, writes):
        need = {}

        def add(tok):
            if tok is None:
                return
            s, v = tok
            if need.get(s, 0) < v:
                need[s] = v

        for ap in reads:
            name, lo, hi = _rng(ap)
            for (l2, h2), rec in self.tr.get(name, {}).items():
                if l2 < hi and lo < h2:
                    add(rec[0])
        for ap in writes:
            name, lo, hi = _rng(ap)
            for (l2, h2), rec in self.tr.get(name, {}).items():
                if l2 < hi and lo < h2:
                    add(rec[0])
                    for s, v in rec[1].items():
                        add((s, v))
        return need

    def _record(self, reads, writes, tok):
        for ap in reads:
            name, lo, hi = _rng(ap)
            d = self.tr.setdefault(name, {})
            rec = d.setdefault((lo, hi), [None, {}])
            if rec[1].get(tok[0], 0) < tok[1]:
                rec[1][tok[0]] = tok[1]
        for ap in writes:
            name, lo, hi = _rng(ap)
            d = self.tr.setdefault(name, {})
            for k in [k for k in d if lo <= k[0] and k[1] <= hi]:
                del d[k]
            d[(lo, hi)] = [tok, {}]

    def op(self, eng, fn, reads=(), writes=(), sem=None, inc=1, signal=True, rec_val=None):
        need = self._need(reads, writes)
        waits = []
        for s, v in need.items():
            if s == "pe" and eng == "pe":
                continue
            if self.waited[eng].get(s, 0) >= v:
                continue
            self.waited[eng][s] = v
            waits.append((s, v))
        tok = None
        if sem is not None:
            self.val[sem] += inc
            tok = (sem, self.val[sem])
            self._record(reads, writes, (sem, rec_val) if rec_val is not None else tok)
        else:
            pr, pw = self.pending[eng]
            pr.extend(reads)
            pw.extend(writes)
            if signal:
                self.val[eng] += inc
                tok = (eng, self.val[eng])
                self._record(pr, pw, tok)
                self.pending[eng] = ([], [])
        self.ops[eng].append((waits, fn, tok, inc))

    def dma_group(self, eng, sem, items):
        allr = [a for _, r, _ in items for a in r]
        allw = [a for _, _, w in items for a in w]
        need = self._need(allr, allw)
        waits = []
        for s, v in need.items():
            if self.waited[eng].get(s, 0) >= v:
                continue
            self.waited[eng][s] = v
            waits.append((s, v))
        final = self.val[sem] + 16 * len(items)
        for i, (fn, r, w) in enumerate(items):
            self.val[sem] += 16
            self.ops[eng].append((waits if i == 0 else [], fn, (sem, self.val[sem]), 16))
        self._record(allr, allw, (sem, final))

    def wait_all(self, eng, toks):
        waits = [(s, v) for s, v in toks if self.waited[eng].get(s, 0) < v]
        for s, v in waits:
            self.waited[eng][s] = v
        self.ops[eng].append((waits, None, None, 0))

    def replay(self, e, eng):
        for waits, fn, tok, inc in self.ops[eng]:
            for s, v in waits:
                e.wait_ge(self.sems[s], v)
            if fn is None:
                continue
            ins = fn(e)
            if tok is not None:
                ins.then_inc(self.sems[tok[0]], inc)


def _split(n, maxn=512):
    k = -(-n // maxn)
    while n % k or (n // k) % 2:
        k += 1
    return k, n // k


def build_program():
    nc = bass.Bass("TRN2", target_bir_lowering=False)
    es = ExitStack()
    with es:
        def din(name, shape):
            return nc.dram_tensor(name, list(shape), F32, kind="ExternalInput").ap()

        xs = din("xs", [TOK_IN, D])
        cvec_d = din("cvec", [128, 8])
        mask_d = din("mask", [128, 1])
        vecs_d = din("vecs", [128, DEPTH, NV])
        rows_d = din("rows", [DEPTH, 128, 2, D])
        bsb_d = din("bsb", [DEPTH, 128, 8, 128])
        wsT_d = din("wsT", [128, DEPTH * 8 * 128])
        bandg_d = din("bandg", [128, 4 * 2 * 128])
        bandf_d = din("bandf", [128, 4 * 2 * 128])
        ident_d = din("ident", [128, 128])
        w_ada = din("w_ada", [DEPTH, D, 6 * D])
        w_in = din("w_in", [DEPTH, D, DIN])
        w_a_out = din("w_a_out", [DEPTH, D, D])
        w_b_out = din("w_b_out", [DEPTH, D, D])
        w_pool = din("w_pool", [DEPTH, 4, 256, 256])
        w_o = din("w_o", [DEPTH, D, D])
        w_up = din("w_up", [DEPTH, D, 2 * DFF])
        w_down = din("w_down", [DEPTH, DFF, D])
        y = nc.dram_tensor("y", [TOK_OUT, D], F32, kind="ExternalOutput").ap()

        def sb(name, shape, dt):
            return es.enter_context(nc.sbuf_tensor(name, list(shape), dt))

        def ps(name, shape):
            return es.enter_context(nc.psum_tensor(name, list(shape), F32))

        xT = sb("xT", [128, 8, TMAX], F32)
        hT = sb("hT", [128, 8, EMAX], BF16)
        preT = sb("preT", [128, 8, EMAX], BF16)
        ubuf = sb("ubuf", [128, 2, EMAX], BF16)
        BIG_MRG = 8 * EMAX * 4
        BIG_BYTES = BIG_MRG + 9 * 1024 * 2
        big = sb("big", [128, BIG_BYTES // 2], BF16)
        tmp = sb("tmp", [128, 3, EMAX], F32)
        gv = tmp[:, 2, 0:1024]
        wsl = sb("wsl", [128, NSLOT, SLOT_ELEMS], BF16)
        bc = sb("bc", [128, 2, D], F32)
        bsb = sb("bsb_s", [128, 8, 128], F32)
        vecs = sb("vecs_s", [128, DEPTH, NV], F32)
        ada = sb("ada", [128, DEPTH, 48], F32)
        der = sb("der", [128, DEPTH, 8, 8], F32)
        cvec = sb("cvec_s", [128, 8], F32)
        cact = sb("cact", [128, 8], BF16)
        mask = sb("mask_s", [128, 1], F32)
        wsT = sb("wsT_s", [128, DEPTH, 8, 128], BF16)
        bandg = sb("bandg_s", [128, 4, 2, 128], BF16)
        bandf = sb("bandf_s", [128, 4, 2, 128], BF16)
        ident = sb("ident_s", [128, 128], F32)
        ones = sb("ones_s", [128, 128], BF16)
        pcar = sb("pcar", [128, DEPTH, D], BF16)
        hcar = sb("hcar", [128, DEPTH, 2, 8, LEAD], BF16)
        stat = sb("stat", [128, 32], F32)
        sc8 = sb("sc8", [128, 16], F32)
        epsc = sb("epsc", [128, 1], F32)
        xin = tmp[:, 0:2, 0:D]
        ost = tmp[:, 0:2, 0:D]

        PA = ps("PA", [128, 3, 512])
        PB = ps("PB", [128, 3, 512])
        PC = ps("PC", [128, 512])
        PD = ps("PD", [128, 512])

        mrg = big[:, 0:BIG_MRG // 2].bitcast(F32).rearrange("p (c t) -> p c t", c=8)
        tok = big[:, BIG_MRG // 2:BIG_BYTES // 2].rearrange("p (b f) -> p b f", b=9)
        fT = big[:, 0:NFC * EMAX].rearrange("p (c t) -> p c t", c=NFC)
        mbf = big[:, BIG_MRG // 2:BIG_BYTES // 2].rearrange("p (c t) -> p c t", c=8)

        R = Rec(nc, es)
        for i in range(NSLOT):
            R.newsem("w%d" % i)
        for n in ["cst", "cstp", "xin0", "xin1", "ost0", "ost1", "bc", "bsb"]:
            R.newsem(n)

        wstate = {"n": 0, "pinned": set(), "last": None}

        def wload(pieces, pin=False):
            while True:
                i = wstate["n"] % NSLOT
                wstate["n"] += 1
                if i not in wstate["pinned"]:
                    break
            if pin:
                wstate["pinned"].add(i)
            wstate["last"] = i
            slot = wsl[:, i, :]
            items = []
            for dstf, src in pieces:
                dst = dstf(slot)
                items.append((lambda e, dst=dst, src=src: e.dma_start(out=dst, in_=src), (), (dst,)))
            R.dma_group("pool", "w%d" % i, items)
            return slot

        def wgrp_k8(src2d, c0, ncols, pin=False):
            src = src2d.rearrange("(kc p) n -> p kc n", p=128)[:, :, c0:c0 + ncols]
            return wload([(lambda s: s[:, 0:8 * ncols].rearrange("p (k n) -> p k n", k=8), src)], pin=pin)

        def pinned_k8(src2d, c0, ncols):
            sl = wgrp_k8(src2d, c0, ncols, pin=True)
            return sl.rearrange("p (k n) -> p k n", k=8), wstate["last"]

        def unpin(*idx):
            for i in idx:
                wstate["pinned"].discard(i)

        pslot = {"n": 0, "m": 0}

        def big_ps():
            pslot["n"] += 1
            return PA if pslot["n"] % 2 else PB

        def small_ps():
            pslot["m"] += 1
            return PC if pslot["m"] % 2 else PD

        def mm(o, l, r, start, stop, signal):
            R.op("pe", lambda e: e.matmul(o, l, r, start=start, stop=stop), reads=(l, r), writes=(o,), signal=signal)

        def fm_proj(lhs_list, rhs_fn, geo, P=None):
            off, nseg, seglen = geo
            if P is None:
                P = big_ps()
            nk = len(lhs_list)
            for j in range(nseg):
                lo = off + j * seglen
                o = P[:, j, 0:seglen]
                for k in range(nk):
                    mm(o, lhs_list[k], rhs_fn(k, lo, lo + seglen), k == 0, k == nk - 1,
                       (j == nseg - 1) and (k == nk - 1))
            return P[:, 0:nseg, 0:seglen]

        def V(l, col, n=1):
            return vecs[:, l, col:col + n]

        def act_op(out, in_, func, bias=0.0, scale=1.0):
            rd = [in_] + [a for a in (bias, scale) if not isinstance(a, float)]
            R.op("act", lambda e: e.activation(out=out, in_=in_, func=func, bias=bias, scale=scale),
                 reads=rd, writes=(out,))

        def tt(out, in0, in1, op, eng="dve"):
            R.op(eng, lambda e: e.tensor_tensor(out=out, in0=in0, in1=in1, op=op), reads=(in0, in1), writes=(out,))

        def ts(out, in0, s1, op0, s2=None, op1=None, eng="dve"):
            rd = [in0] + [a for a in (s1, s2) if a is not None and not isinstance(a, float)]
            if op1 is None:
                R.op(eng, lambda e: e.tensor_scalar(out=out, in0=in0, scalar1=s1, scalar2=None, op0=op0),
                     reads=rd, writes=(out,))
            else:
                R.op(eng, lambda e: e.tensor_scalar(out=out, in0=in0, scalar1=s1, scalar2=s2, op0=op0, op1=op1),
                     reads=rd, writes=(out,))

        def stt(out, in0, scalar, in1, op0, op1, eng="dve"):
            rd = [in0, in1] + ([] if isinstance(scalar, float) else [scalar])
            R.op(eng, lambda e: e.scalar_tensor_tensor(out=out, in0=in0, scalar=scalar, in1=in1, op0=op0, op1=op1),
                 reads=rd, writes=(out,))

        def cp(out, in_, eng="dve"):
            R.op(eng, lambda e: e.tensor_copy(out=out, in_=in_), reads=(in_,), writes=(out,))

        def mset(ap, val, eng="dve"):
            R.op(eng, lambda e: e.memset(ap, val), writes=(ap,))

        def dma(eng, out, in_, sem, reads=(), writes=()):
            R.op(eng, lambda e: e.dma_start(out=out, in_=in_), reads=reads, writes=writes, sem=sem, inc=16)

        def v3(ap2d, nseg, seglen):
            return ap2d.rearrange("p (s n) -> p s n", s=nseg)

        dma("sp", vecs[:], vecs_d, "cst", writes=(vecs[:],))
        dma("sp", cvec[:], cvec_d, "cst", writes=(cvec[:],))
        dma("sp", mask[:], mask_d, "cst", writes=(mask[:],))
        dma("sp", ident[:], ident_d, "cst", writes=(ident[:],))
        for l in range(DEPTH):
            dma("pool", wsT[:, l].rearrange("p b c -> p (b c)"), wsT_d[:, l * 1024:(l + 1) * 1024], "cstp", writes=(wsT[:, l],))
        dma("pool", bandg[:].rearrange("p a b c -> p (a b c)"), bandg_d, "cstp", writes=(bandg[:],))
        dma("pool", bandf[:].rearrange("p a b c -> p (a b c)"), bandf_d, "cstp", writes=(bandf[:],))
        for name in list(R.tr.keys()):
            for k, rec in R.tr[name].items():
                if rec[0] is not None and rec[0][0] in ("cst", "cstp"):
                    rec[0] = (rec[0][0], R.val[rec[0][0]])

        mset(ones[:], 1.0)
        mset(epsc[:], EPS)
        mset(hT[:], 0.0)
        mset(pcar[:], 0.0)
        mset(hcar[:], 0.0)
        mset(tmp[:], 0.0)
        for l in range(DEPTH):
            R.op("dve", lambda e, l=l: e.memset(wsT[64:128, l, :, 0:64], 0.0), writes=(wsT[:, l, :, :],))
        act_op(cact[:], cvec[:], AF.Silu)

        def ada_step(l, g):
            Pc = small_ps()
            slot = wgrp_k8(w_ada[l], g * 512, 512)
            wv = slot.rearrange("p (k n) -> p k n", k=8)
            for jj in range(4):
                for k in range(8):
                    mm(Pc[:, jj:jj + 1], wv[:, k, jj * 128:(jj + 1) * 128], cact[:, k:k + 1], k == 0, k == 7, k == 7)
            tt(ada[:, l, 4 * g:4 * g + 4], Pc[:, 0:4], V(l, V_BADA + 4 * g, 4), ALU.add)

        def der_part1(l):
            sh1, sc1 = ada[:, l, 0:8], ada[:, l, 8:16]
            t1 = sc8[:, 0:8]
            ts(t1, sc1, 1.0, ALU.add)
            if l == 0:
                ts(der[:, l, 0, :], t1, 1.0 / ALPHA, ALU.mult)
                cp(der[:, l, 1, :], sh1)
            else:
                tt(der[:, l, 0, :], t1, V(l - 1, V_LN2G, 8), ALU.mult)
                tt(der[:, l, 1, :], t1, V(l - 1, V_LN2B, 8), ALU.mult)
                tt(der[:, l, 1, :], der[:, l, 1, :], sh1, ALU.add)

        def der_part2(l):
            sh2, sc2 = ada[:, l, 24:32], ada[:, l, 32:40]
            t2 = sc8[:, 8:16]
            ts(t2, sc2, 1.0, ALU.add)
            tt(der[:, l, 2, :], t2, V(l, V_LN1G, 8), ALU.mult)
            tt(der[:, l, 3, :], t2, V(l, V_LN1B, 8), ALU.mult)
            tt(der[:, l, 3, :], der[:, l, 3, :], sh2, ALU.add)
            ts(der[:, l, 4, :], V(l, V_LN1G, 8), ALPHA, ALU.mult)
            ts(der[:, l, 5, :], V(l, V_LN1B, 8), ALPHA, ALU.mult)
            ts(der[:, l, 6, :], V(l, V_LN2G, 8), ALPHA, ALU.mult)
            ts(der[:, l, 7, :], V(l, V_LN2B, 8), ALPHA, ALU.mult)

        todo = []

        def tick(n=1):
            for _ in range(n):
                if todo:
                    todo.pop(0)()

        def drain():
            while todo:
                todo.pop(0)()

        def layer_norm_fm(c0, T, outs):
            nseg, seglen = _split(T)
            S1 = big_ps()
            S2 = big_ps()
            for c in range(8):
                r = xT[:, c, c0:c0 + T]
                rbc = ubuf[:, 0, 0:T]
                rsc = ubuf[:, 1, 0:T]
                cp(rbc, r)
                act_op(rsc, r, AF.Square)
                for j in range(nseg):
                    lo = j * seglen
                    mm(S1[:, j, 0:seglen], ones[:], rbc[:, lo:lo + seglen], c == 0, c == 7, False)
                    mm(S2[:, j, 0:seglen], ones[:], rsc[:, lo:lo + seglen], c == 0, c == 7, j == nseg - 1)
            mean = v3(tmp[:, 1, 0:T], nseg, seglen)
            rstd = v3(tmp[:, 2, 0:T], nseg, seglen)
            act_op(rstd, S1[:, 0:nseg, 0:seglen], AF.Square, scale=1.0 / D)
            act_op(mean, S1[:, 0:nseg, 0:seglen], AF.Identity, scale=1.0 / D)
            stt(rstd, S2[:, 0:nseg, 0:seglen], 1.0 / D, rstd, ALU.mult, ALU.subtract)
            act_op(rstd, rstd, AF.Sqrt, bias=epsc[:, 0:1])
            R.op("dve", lambda e: e.reciprocal(out=rstd, in_=rstd), reads=(rstd,), writes=(rstd,))
            meanf, rstdf = tmp[:, 1, 0:T], tmp[:, 2, 0:T]
            for c in range(8):
                r = xT[:, c, c0:c0 + T]
                yh = tmp[:, 0, 0:T] if c % 2 == 0 else ubuf[:].rearrange("p a b -> p (a b)").bitcast(F32)[:, 0:T]
                tt(yh, r, meanf, ALU.subtract)
                tt(yh, yh, rstdf, ALU.mult)
                for dst_fn, sc_fn, b_fn in outs:
                    act_op(dst_fn(c), yh, AF.Identity, bias=b_fn(c), scale=sc_fn(c))

        def geom(c0, T):
            for lead in (2, 4, 6):
                if _split(T + lead)[0] <= 3:
                    break
            else:
                raise AssertionError("no segment geometry for T=%d" % T)
            assert _split(T)[0] <= 3
            E = T + lead
            nse, sle = _split(E)
            ns, sl = _split(T)
            hcol0 = LEAD + c0
            return lead, E, nse, sle, ns, sl, hcol0

        def mixing(l, c0, T, gtile0, first_tile):
            tb0, off = c0 // 128, c0 % 128
            nb = 9 - tb0
            assert off + T == nb * 128
            tcol0 = LEAD + tb0 * 128
            lead, E, nse, sle, ns, sl, hcol0 = geom(c0, T)
            geoE = (hcol0 - lead, nse, sle)
            geoP = (0, nse, sle)
            geoT = (lead, ns, sl)
            cp(hT[:, :, 0:LEAD], hcar[:, l, 0, :, :])
            cp(hcar[:, l, 0, :, :], hT[:, :, LEAD + TMAX - LEAD:LEAD + TMAX])
            dma("sp", bc[:], rows_d[l], "bc", writes=(bc[:],))
            dma("sp", bsb[:], bsb_d[l], "bsb", writes=(bsb[:],))
            for hh in range(2):
                Pr = small_ps()
                mm(Pr[:], ones[:], wsT[:, l, 4 * hh:4 * hh + 4, :].rearrange("p a b -> p (a b)"), True, True, True)
                for gg in range(4):
                    g = 4 * hh + gg
                    stt(bsb[:, g, :], Pr[:, gg * 128:(gg + 1) * 128], V(l, V_LNVB + g), bsb[:, g, :], ALU.mult, ALU.add)
            nmask = (lead + max(0, 256 - (gtile0 * 128 + c0))) if first_tile else 0
            win_v = w_in[l].rearrange("(kc p) (r n) -> p kc r n", p=128, n=128)

            def hrhs(k, lo, hi):
                return hT[:, k, lo:hi]

            def prhs(k, lo, hi):
                return preT[:, k, lo:hi]

            def gated_out(wmat, gate_col0, mode):
                for g2 in range(2):
                    gslot, gi = pinned_k8(w_in[l], gate_col0 + g2 * 512, 512)
                    wslot, wi = pinned_k8(wmat, g2 * 512, 512)
                    for jj in range(4):
                        j = g2 * 4 + jj
                        Pg = fm_proj([gslot[:, k, jj * 128:(jj + 1) * 128] for k in range(8)], hrhs, geoE)
                        gate = v3(tmp[:, 0, 0:E], nse, sle)
                        act_op(gate, Pg, AF.Sigmoid, bias=V(l, V_BIN + gate_col0 // 128 + j))
                        Py = fm_proj([wslot[:, k, jj * 128:(jj + 1) * 128] for k in range(8)], prhs, geoP)
                        mj = v3(mrg[:, j, 0:E], nse, sle)
                        if mode == "set":
                            tt(mj, Py, gate, ALU.mult)
                        else:
                            t1 = v3(tmp[:, 1, 0:E], nse, sle)
                            tt(t1, Py, gate, ALU.mult)
                            tt(mj, mj, t1, ALU.add)
                        tick()
                    unpin(gi, wi)

            for c in range(8):
                slot = wload([
                    (lambda s, r=r: s[:, 0:3072].rearrange("p (k r n) -> p k r n", k=8, r=3)[:, :, r, :],
                     win_v[:, :, c + 8 * r, :]) for r in range(3)
                ])
                sv = slot[:, 0:3072].rearrange("p (k r n) -> p k r n", k=8, r=3)
                Pzx = fm_proj([sv[:, k, 2, :] for k in range(8)], hrhs, geoE)
                Pzc = fm_proj([sv[:, k, 1, :] for k in range(8)], hrhs, geoE)
                zx = v3(tmp[:, 0, 0:E], nse, sle)
                act_op(zx, Pzx, AF.Identity, bias=V(l, V_BIN + 16 + c))
                prodf = tmp[:, 1, 0:E]
                stt(v3(prodf, nse, sle), Pzc, V(l, V_BIN + 8 + c), zx, ALU.add, ALU.mult)
                if nmask:
                    ts(prodf[:, 0:nmask], prodf[:, 0:nmask], mask[:, 0:1], ALU.mult)
                Pzb = fm_proj([sv[:, k, 0, :] for k in range(8)], hrhs, geoE)
                cv = tmp[:, 2, 0:E]
                act_op(cv, prodf, AF.Identity, scale=V(l, V_CONVA + 16 + c))
                stt(cv[:, 2:E], prodf[:, 1:E - 1], V(l, V_CONVA + 8 + c), cv[:, 2:E], ALU.mult, ALU.add)
                stt(cv[:, 2:E], prodf[:, 0:E - 2], V(l, V_CONVA + 0 + c), cv[:, 2:E], ALU.mult, ALU.add)
                stt(v3(preT[:, c, 0:E], nse, sle), Pzb, V(l, V_BIN + c), v3(cv, nse, sle), ALU.add, ALU.mult)
                tick()
            gated_out(w_a_out[l], 6144, "set")

            zvslots = [wgrp_k8(w_in[l], 4096 + hf * 512, 512).rearrange("p (k n) -> p k n", k=8) for hf in range(2)]
            def v_s1(b):
                gv = tmp[:, 1 + b % 2, 0:1024]
                st = stat[:, (b % 2) * 16:(b % 2) * 16 + 16]
                for hf in range(2):
                    Pt = small_ps()
                    for k in range(8):
                        mm(Pt[:], hT[:, k, tcol0 + b * 128:tcol0 + (b + 1) * 128], zvslots[hf][:, k, :], k == 0, k == 7, k == 7)
                    gsl = gv[:, hf * 512:(hf + 1) * 512]
                    tt(gsl, Pt[:], bc[:, 0, hf * 512:(hf + 1) * 512], ALU.add)
                    act_op(gsl, gsl, AF.Gelu_apprx_tanh)
                for hf in range(2):
                    gsl = gv[:, hf * 512:(hf + 1) * 512]
                    R.op("dve", lambda e, hf=hf, gsl=gsl, st=st: e.bn_stats(out=st[:, hf * 6:(hf + 1) * 6], in_=gsl),
                         reads=(gsl,), writes=(st[:, hf * 6:(hf + 1) * 6],))

            def v_s2(b):
                gv = tmp[:, 1 + b % 2, 0:1024]
                st = stat[:, (b % 2) * 16:(b % 2) * 16 + 16]
                R.op("dve", lambda e: e.bn_aggr(out=st[:, 12:14], in_=st[:, 0:12].rearrange("p (a b) -> p a b", a=2)),
                     reads=(st[:, 0:12],), writes=(st[:, 12:14],))
                act_op(st[:, 14:15], st[:, 13:14], AF.Sqrt, bias=epsc[:, 0:1])
                R.op("dve", lambda e: e.reciprocal(out=st[:, 14:15], in_=st[:, 14:15]),
                     reads=(st[:, 14:15],), writes=(st[:, 14:15],))
                ts(tok[:, b, :], gv, st[:, 12:13], ALU.subtract, st[:, 14:15], ALU.mult)

            v_s1(0)
            for b in range(1, nb):
                v_s1(b)
                v_s2(b - 1)
            v_s2(nb - 1)
            for g2 in range(2):
                uslot, ui = pinned_k8(w_in[l], 3072 + g2 * 512, 512)
                for jj in range(4):
                    g = g2 * 4 + jj
                    Pu = fm_proj([uslot[:, k, jj * 128:(jj + 1) * 128] for k in range(8)], hrhs, geoE)
                    u = ubuf[:, g % 2, 0:E]
                    act_op(v3(u, nse, sle), Pu, AF.Gelu_apprx_tanh, bias=V(l, V_BIN + 24 + g))
                    Pm = big_ps()
                    Pmf = Pm[:].rearrange("p a b -> p (a b)")
                    for b in range(nb):
                        mm(Pmf[:, b * 128:(b + 1) * 128], tok[:, b, g * 128:(g + 1) * 128], wsT[:, l, g, :],
                           True, True, b == nb - 1)
                    mx = tmp[:, 0, 0:nb * 128]
                    stt(mx.rearrange("p (b i) -> p b i", b=nb), Pmf[:, 0:nb * 128].rearrange("p (b i) -> p b i", b=nb),
                        V(l, V_LNVG + g), bsb[:, g:g + 1, :].broadcast_to([128, nb, 128]), ALU.mult, ALU.add)
                    mset(preT[:, g, 0:lead], 0.0)
                    tt(preT[:, g, lead:E], mx[:, off:off + T], u[:, lead:E], ALU.mult)
                    tick()
                unpin(ui)
            gated_out(w_b_out[l], 7168, "add")

            zpslots = [wgrp_k8(w_in[l], 5120 + hf * 512, 512).rearrange("p (k n) -> p k n", k=8) for hf in range(2)]
            for b in range(nb):
                for hf in range(2):
                    Pt = small_ps()
                    for k in range(8):
                        mm(Pt[:], hT[:, k, tcol0 + b * 128:tcol0 + (b + 1) * 128], zpslots[hf][:, k, :], k == 0, k == 7, k == 7)
                    tt(tok[:, b, hf * 512:(hf + 1) * 512], Pt[:], bc[:, 1, hf * 512:(hf + 1) * 512], ALU.add)
            for c in range(8):
                kw = c // 2
                Pm = big_ps()
                Pmf = Pm[:].rearrange("p a b -> p (a b)")
                for b in range(nb):
                    band = bandf if (gtile0 + tb0 + b) == 2 else bandg
                    prev = pcar[:, l, c * 128:(c + 1) * 128] if b == 0 else tok[:, b - 1, c * 128:(c + 1) * 128]
                    o = Pmf[:, b * 128:(b + 1) * 128]
                    mm(o, tok[:, b, c * 128:(c + 1) * 128], band[:, kw, 0, :], True, False, False)
                    mm(o, prev, band[:, kw, 1, :], False, True, b == nb - 1)
                act_op(preT[:, c, lead:E], Pmf[:, off:off + T], AF.Identity)
            cp(pcar[:, l, :], tok[:, nb - 1, :])
            pslot_w = wload([
                (lambda s: s[:, 0:2048].rearrange("p (wk n) -> p wk n", wk=8),
                 w_pool[l].rearrange("w (kc p) n -> p (w kc) n", p=128)),
            ])[:, 0:2048].rearrange("p (w k n) -> p w k n", w=4, k=2)
            for g2 in range(2):
                gslot = wgrp_k8(w_in[l], 8192 + g2 * 512, 512).rearrange("p (k n) -> p k n", k=8)
                for jj in range(4):
                    j = g2 * 4 + jj
                    kw = j // 2
                    Pg = fm_proj([gslot[:, k, jj * 128:(jj + 1) * 128] for k in range(8)], hrhs, geoE)
                    gate = v3(tmp[:, 0, 0:E], nse, sle)
                    act_op(gate, Pg, AF.Sigmoid, bias=V(l, V_BIN + 64 + j))
                    Py = fm_proj([pslot_w[:, kw, kc, (j % 2) * 128:(j % 2 + 1) * 128] for kc in range(2)],
                                 lambda k, lo, hi, kw=kw: preT[:, 2 * kw + k, lo:hi], geoP)
                    t1 = v3(tmp[:, 1, 0:E], nse, sle)
                    stt(t1, Py, V(l, V_PSC + j), gate, ALU.mult, ALU.mult)
                    tt(mbf[:, j, 0:T], mrg[:, j, lead:E], tmp[:, 1, lead:E], ALU.add)

            drain()
            for g2 in range(2):
                wslot = wgrp_k8(w_o[l], g2 * 512, 512).rearrange("p (k n) -> p k n", k=8)
                for jj in range(4):
                    j = g2 * 4 + jj
                    Po = fm_proj([wslot[:, k, jj * 128:(jj + 1) * 128] for k in range(8)],
                                 lambda k, lo, hi: mbf[:, k, lo:hi], (0, ns, sl))
                    xv = v3(xT[:, j, c0:c0 + T], ns, sl)
                    stt(xv, Po, ada[:, l, 16 + j:17 + j], xv, ALU.mult, ALU.add)
            layer_norm_fm(c0, T, [
                (lambda c: xT[:, c, c0:c0 + T], lambda c: der[:, l, 4, c:c + 1], lambda c: der[:, l, 5, c:c + 1]),
                (lambda c: hT[:, c, hcol0:hcol0 + T], lambda c: der[:, l, 2, c:c + 1], lambda c: der[:, l, 3, c:c + 1]),
            ])

        def ffn(l, c0, T, gtile0, first_tile):
            lead, E, nse, sle, ns, sl, hcol0 = geom(c0, T)
            geoE = (hcol0 - lead, nse, sle)
            cp(hT[:, :, 0:LEAD], hcar[:, l, 1, :, :])
            cp(hcar[:, l, 1, :, :], hT[:, :, LEAD + TMAX - LEAD:LEAD + TMAX])
            nmask = (lead + max(0, 256 - (gtile0 * 128 + c0))) if first_tile else 0
            wup_v = w_up[l].rearrange("(kc p) (r n) -> p kc r n", p=128, r=2)

            def hrhs(k, lo, hi):
                return hT[:, k, lo:hi]

            for i in range(NFC // 2):
                slot = wload([
                    (lambda s, r=r: s[:, 0:4096].rearrange("p (k r n) -> p k r n", k=8, r=2)[:, :, r, :],
                     wup_v[:, :, r, i * 256:(i + 1) * 256]) for r in range(2)
                ])
                sv = slot[:, 0:4096].rearrange("p (k r n) -> p k r n", k=8, r=2)
                for jj in range(2):
                    c = 2 * i + jj
                    Pa = fm_proj([sv[:, k, 0, jj * 128:(jj + 1) * 128] for k in range(8)], hrhs, geoE)
                    af = tmp[:, 0, 0:E]
                    act_op(v3(af, nse, sle), Pa, AF.Identity, bias=V(l, V_BUP + c))
                    if nmask:
                        ts(af[:, 0:nmask], af[:, 0:nmask], mask[:, 0:1], ALU.mult)
                    cv = tmp[:, 1, 0:E]
                    act_op(cv, af, AF.Identity, scale=V(l, V_CONVF + 44 + c))
                    stt(cv[:, 2:E], af[:, 1:E - 1], V(l, V_CONVF + 22 + c), cv[:, 2:E], ALU.mult, ALU.add)
                    stt(cv[:, 2:E], af[:, 0:E - 2], V(l, V_CONVF + 0 + c), cv[:, 2:E], ALU.mult, ALU.add)
                    gl = tmp[:, 2, 0:E]
                    act_op(gl[:, 2:E], cv[:, 2:E], AF.Gelu_apprx_tanh, bias=V(l, V_CONVFB + c))
                    Pg = fm_proj([sv[:, k, 1, jj * 128:(jj + 1) * 128] for k in range(8)], hrhs, geoE)
                    stt(v3(fT[:, c, 0:E], nse, sle), Pg, V(l, V_BUP + NFC + c), v3(gl, nse, sle), ALU.add, ALU.mult)
            wd_v = w_down[l].rearrange("(kc p) n -> p kc n", p=128)
            for jp in range(4):
                Ps = [big_ps(), big_ps()]
                for kh in range(2):
                    slot = wload([
                        (lambda s: s[:, 0:2816].rearrange("p (k n) -> p k n", k=11),
                         wd_v[:, kh * 11:(kh + 1) * 11, jp * 256:(jp + 1) * 256]),
                    ])
                    sv = slot[:, 0:2816].rearrange("p (k n) -> p k n", k=11)
                    for jj in range(2):
                        for s in range(ns):
                            o = Ps[jj][:, s, 0:sl]
                            for k in range(11):
                                mm(o, sv[:, k, jj * 128:(jj + 1) * 128],
                                   fT[:, kh * 11 + k, lead + s * sl:lead + (s + 1) * sl],
                                   kh == 0 and k == 0, kh == 1 and k == 10, s == ns - 1 and k == 10)
                for jj in range(2):
                    j = jp * 2 + jj
                    xv = v3(xT[:, j, c0:c0 + T], ns, sl)
                    stt(xv, Ps[jj][:, 0:ns, 0:sl], ada[:, l, 40 + j:41 + j], xv, ALU.mult, ALU.add)
            if l < DEPTH - 1:
                layer_norm_fm(c0, T, [
                    (lambda c: xT[:, c, c0:c0 + T], lambda c: der[:, l, 6, c:c + 1], lambda c: der[:, l, 7, c:c + 1]),
                    (lambda c: hT[:, c, hcol0:hcol0 + T], lambda c: der[:, l + 1, 0, c:c + 1], lambda c: der[:, l + 1, 1, c:c + 1]),
                ])
            else:
                layer_norm_fm(c0, T, [
                    (lambda c: xT[:, c, c0:c0 + T], lambda c: V(l, V_LN2G + c), lambda c: V(l, V_LN2B + c)),
                ])

        iost = {"i": 0, "o": 0}

        def load_x(gblk_list):
            for bi, gb in enumerate(gblk_list):
                s = iost["i"] % 2
                iost["i"] += 1
                dma("sp", xin[:, s, :], xs[gb * 128:(gb + 1) * 128, :], "xin%d" % s, writes=(xin[:, s, :],))
                for half, P in enumerate((PC, PD)):
                    for cc in range(4):
                        c = half * 4 + cc
                        o = P[:, cc * 128:(cc + 1) * 128]
                        i_ap = xin[:, s, c * 128:(c + 1) * 128]
                        R.op("pe", lambda e, o=o, i_ap=i_ap: e.transpose(o, i_ap, ident[:]),
                             reads=(i_ap, ident[:]), writes=(o,), signal=(cc == 3))
                    ts(xT[:, half * 4:half * 4 + 4, bi * 128:(bi + 1) * 128],
                       P[:, 0:512].rearrange("p (c t) -> p c t", c=4), ALPHA, ALU.mult)

        def make_h0(T):
            for c in range(8):
                if c % 2 == 0:
                    act_op(hT[:, c, LEAD:LEAD + T], xT[:, c, 0:T], AF.Identity,
                           bias=der[:, 0, 1, c:c + 1], scale=der[:, 0, 0, c:c + 1])
                else:
                    ts(hT[:, c, LEAD:LEAD + T], xT[:, c, 0:T], der[:, 0, 0, c:c + 1], ALU.mult,
                       der[:, 0, 1, c:c + 1], ALU.add)

        def store_y(col_blocks):
            for b, ob in col_blocks:
                s = iost["o"] % 2
                iost["o"] += 1
                for half, P in enumerate((PC, PD)):
                    for cc in range(4):
                        c = half * 4 + cc
                        o = P[:, cc * 128:(cc + 1) * 128]
                        i_ap = xT[:, c, b * 128:(b + 1) * 128]
                        R.op("pe", lambda e, o=o, i_ap=i_ap: e.transpose(o, i_ap, ident[:]),
                             reads=(i_ap, ident[:]), writes=(o,), signal=(cc == 3))
                    if half == 0:
                        act_op(ost[:, s, 0:512], P[:, 0:512], AF.Identity)
                    else:
                        cp(ost[:, s, 512:1024], P[:, 0:512])
                dma("sp", y[ob * 128:(ob + 1) * 128, :], ost[:, s, :], "ost%d" % s, reads=(ost[:, s, :],))

        load_x(list(range(0, 9)))
        for g in range(4):
            ada_step(0, g)
        der_part1(0)
        make_h0(1152)
        for g in range(4, 12):
            todo.append(lambda g=g: ada_step(0, g))
        todo.append(lambda: der_part2(0))
        for g in range(12):
            todo.append(lambda g=g: ada_step(1, g))
        todo.append(lambda: der_part1(1))
        todo.append(lambda: der_part2(1))
        mixing(0, 120, 1032, 0, True)
        ffn(0, 120, 1032, 0, True)
        mixing(1, 248, 904, 0, True)
        ffn(1, 248, 904, 0, True)
        store_y([(b, b - 2) for b in range(2, 9)])
        load_x(list(range(9, 18)))
        make_h0(1152)
        mixing(0, 0, 1152, 9, False)
        ffn(0, 0, 1152, 9, False)
        mixing(1, 0, 1152, 9, False)
        ffn(1, 0, 1152, 9, False)
        store_y([(b, 7 + b) for b in range(0, 9)])
        R.wait_all("sp", [("ost0", R.val["ost0"]), ("ost1", R.val["ost1"])])

        with nc.Block() as block:
            @block.tensor
            def _(e):
                R.replay(e, "pe")

            @block.scalar
            def _(e):
                R.replay(e, "act")

            @block.vector
            def _(e):
                R.replay(e, "dve")

            @block.gpsimd
            def _(e):
                R.replay(e, "pool")

            @block.sync
            def _(e):
                R.replay(e, "sp")
    return nc


def _fm(v, nch):
    return np.ascontiguousarray(np.asarray(v, np.float32).reshape(nch, 128).T)


def _bands(first):
    out = np.zeros((128, 4, 2, 128), np.float32)
    for k, w in enumerate(POOL_W):
        for t in range(128):
            if first:
                n = min(t + 1, w)
                for tp in range(max(0, t - w + 1), t + 1):
                    out[tp, k, 0, t] += 1.0 / n
            else:
                for tp in range(t - w + 1, t + 1):
                    if tp >= 0:
                        out[tp, k, 0, t] += 1.0 / w
                    else:
                        out[tp + 128, k, 1, t] += 1.0 / w
            out[t, k, 0, t] -= 1.0
    return out.reshape(128, 4 * 2 * 128)


_CACHE = {}


def kernel(x, c, w_ada, b_ada, w_in, b_in, conv_a, w_a_out, ln_v_g, ln_v_b,
           w_spatial, b_spatial, w_b_out, w_pool, pool_scale, w_o, ln1_g, ln1_b,
           w_up, b_up, conv_ffn, conv_ffn_b, w_down, ln2_g, ln2_b):
    f = lambda a: np.ascontiguousarray(np.asarray(a, dtype=np.float32))
    x = f(x)
    c = f(c)
    vecs = np.zeros((128, DEPTH, NV), np.float32)
    rows = np.zeros((DEPTH, 128, 2, D), np.float32)
    bsb = np.zeros((DEPTH, 128, 8, 128), np.float32)
    wsT = np.zeros((128, DEPTH, 8, 128), np.float32)
    for l in range(DEPTH):
        vecs[:, l, V_BIN:V_BIN + 72] = _fm(b_in[l], 72)
        vecs[:, l, V_BADA:V_BADA + 48] = _fm(b_ada[l], 48)
        vecs[:, l, V_CONVA:V_CONVA + 24] = np.concatenate([_fm(conv_a[l][k], 8) for k in range(3)], axis=1)
        vecs[:, l, V_LN1G:V_LN1G + 8] = _fm(ln1_g[l], 8)
        vecs[:, l, V_LN1B:V_LN1B + 8] = _fm(ln1_b[l], 8)
        vecs[:, l, V_LN2G:V_LN2G + 8] = _fm(ln2_g[l], 8)
        vecs[:, l, V_LN2B:V_LN2B + 8] = _fm(ln2_b[l], 8)
        vecs[:, l, V_PSC:V_PSC + 8] = _fm(pool_scale[l], 8)
        vecs[:, l, V_BUP:V_BUP + 44] = _fm(b_up[l], 44)
        vecs[:, l, V_CONVF:V_CONVF + 66] = np.concatenate([_fm(conv_ffn[l][k], NFC) for k in range(3)], axis=1)
        vecs[:, l, V_CONVFB:V_CONVFB + NFC] = _fm(conv_ffn_b[l], NFC)
        vecs[:, l, V_LNVG:V_LNVG + 8] = _fm(ln_v_g[l], 8)
        vecs[:, l, V_LNVB:V_LNVB + 8] = _fm(ln_v_b[l], 8)
        bl = np.asarray(b_in[l], np.float32)
        rows[l, :, 0, :] = bl[4096:5120][None, :]
        rows[l, :, 1, :] = bl[5120:6144][None, :]
        bsb[l] = np.asarray(b_spatial[l], np.float32)[None, :, :]
        wsT[:, l] = np.transpose(np.asarray(w_spatial[l], np.float32), (2, 0, 1))
    wsT = np.ascontiguousarray(wsT.reshape(128, DEPTH * 8 * 128))
    bandg = _bands(False)
    bandf = _bands(True)
    ident = np.eye(128, dtype=np.float32)
    shared = {
        "vecs": vecs, "rows": rows, "bsb": bsb, "wsT": wsT, "bandg": bandg, "ident": ident,
        "w_ada": f(w_ada), "w_in": f(w_in), "w_a_out": f(w_a_out), "w_b_out": f(w_b_out),
        "w_pool": f(w_pool), "w_o": f(w_o), "w_up": f(w_up), "w_down": f(w_down),
    }
    in_maps = []
    for core in range(8):
        b, half = core // 2, core % 2
        xs = np.zeros((TOK_IN, D), np.float32)
        if half == 0:
            xs[256:] = x[b, 0:2048]
        else:
            xs[:] = x[b, 2048 - 256:4096]
        m = dict(shared)
        m["xs"] = xs
        m["cvec"] = _fm(c[b], 8)
        m["mask"] = np.full((128, 1), float(half), np.float32)
        m["bandf"] = bandf if half == 0 else bandg
        in_maps.append(m)
    if "nc" not in _CACHE:
        _CACHE["nc"] = build_program()
    res = run_bass_kernel_spmd(_CACHE["nc"], in_maps, core_ids=list(range(8)))
    out = np.zeros((NBATCH, SEQ, D), np.float32)
    for core in range(8):
        b, half = core // 2, core % 2
        out[b, half * 2048:(half + 1) * 2048] = res.results[core]["y"]
    return out
```

```python
import numpy as np
from contextlib import ExitStack
import concourse.bass as bass
import concourse.mybir as mybir
from concourse.bass_utils import run_bass_kernel_spmd

F32 = mybir.dt.float32
BF16 = mybir.dt.bfloat16
AF = mybir.ActivationFunctionType
ALU = mybir.AluOpType
ESZ = {F32: 4, BF16: 2}
PSUM_NAMES = ("PA", "PB", "PC", "PD")

D = 1024
SEQ = 4096
NBATCH = 4
DEPTH = 2
DFF = 2816
NFC = DFF // 128
DIN = 9216
ALPHA = (2 * DEPTH) ** 0.25
EPS = 1e-5
POOL_W = (2, 4, 8, 16)
NBLK_IN = 18
TOK_IN = NBLK_IN * 128
TOK_OUT = 2048
TMAX = 1152
LEAD = 6
EMAX = TMAX + LEAD
NSLOT = 4
SLOT_ELEMS = 4096

V_BIN = 0
V_BADA = 72
V_CONVA = 120
V_LN1G = 144
V_LN1B = 152
V_LN2G = 160
V_LN2B = 168
V_PSC = 176
V_BUP = 184
V_CONVF = 228
V_CONVFB = 294
V_LNVG = 316
V_LNVB = 324
NV = 332


def _rng(ap):
    a = ap.ap
    esz = ESZ[ap.dtype]
    pstep = a[0][0]
    off = ap.offset % pstep if pstep > 0 else ap.offset
    span = 1
    for step, cnt in a[1:]:
        span += (cnt - 1) * abs(step)
    lo, hi = off * esz, (off + span) * esz
    if ap.tensor.name in PSUM_NAMES:
        lo = (lo // 2048) * 2048
        hi = -(-hi // 2048) * 2048
    return ap.tensor.name, lo, hi


class Rec:
    def __init__(self, nc, es):
        self.nc = nc
        self.es = es
        self.engs = ["pe", "act", "dve", "pool", "sp"]
        self.sems = {}
        self.val = {}
        self.waited = {e: {} for e in self.engs}
        self.ops = {e: [] for e in self.engs}
        self.tr = {}
        self.pending = {e: ([], []) for e in self.engs}
        for e in self.engs:
            self.newsem(e)

    def newsem(self, name):
        self.sems[name] = self.es.enter_context(self.nc.semaphore("s_" + name))
        self.val[name] = 0

    def _need(self, reads, writes):
        need = {}

        def add(tok):
            if tok is None:
                return
            s, v = tok
            if need.get(s, 0) < v:
                need[s] = v

        for ap in reads:
            name, lo, hi = _rng(ap)
            for (l2, h2), rec in self.tr.get(name, {}).items():
                if l2 < hi and lo < h2:
                    add(rec[0])
        for ap in writes:
            name, lo, hi = _rng(ap)
            for (l2, h2), rec in self.tr.get(name, {}).items():
                if l2 < hi and lo < h2:
                    add(rec[0])
                    for s, v in rec[1].items():
                        add((s, v))
        return need

    def _record(self, reads, writes, tok):
        for ap in reads:
            name, lo, hi = _rng(ap)
            d = self.tr.setdefault(name, {})
            rec = d.setdefault((lo, hi), [None, {}])
            if rec[1].get(tok[0], 0) < tok[1]:
                rec[1][tok[0]] = tok[1]
        for ap in writes:
            name, lo, hi = _rng(ap)
            d = self.tr.setdefault(name, {})
            for k in [k for k in d if lo <= k[0] and k[1] <= hi]:
                del d[k]
            d[(lo, hi)] = [tok, {}]

    def op(self, eng, fn, reads=(), writes=(), sem=None, inc=1, signal=True, rec_val=None):
        need = self._need(reads, writes)
        waits = []
        for s, v in need.items():
            if s == "pe" and eng == "pe":
                continue
            if self.waited[eng].get(s, 0) >= v:
                continue
            self.waited[eng][s] = v
            waits.append((s, v))
        tok = None
        if sem is not None:
            self.val[sem] += inc
            tok = (sem, self.val[sem])
            self._record(reads, writes, (sem, rec_val) if rec_val is not None else tok)
        else:
            pr, pw = self.pending[eng]
            pr.extend(reads)
            pw.extend(writes)
            if signal:
                self.val[eng] += inc
                tok = (eng, self.val[eng])
                self._record(pr, pw, tok)
                self.pending[eng] = ([], [])
        self.ops[eng].append((waits, fn, tok, inc))

    def dma_group(self, eng, sem, items):
        allr = [a for _, r, _ in items for a in r]
        allw = [a for _, _, w in items for a in w]
        need = self._need(allr, allw)
        waits = []
        for s, v in need.items():
            if self.waited[eng].get(s, 0) >= v:
                continue
            self.waited[eng][s] = v
            waits.append((s, v))
        final = self.val[sem] + 16 * len(items)
        for i, (fn, r, w) in enumerate(items):
            self.val[sem] += 16
            self.ops[eng].append((waits if i == 0 else [], fn, (sem, self.val[sem]), 16))
        self._record(allr, allw, (sem, final))

    def wait_all(self, eng, toks):
        waits = [(s, v) for s, v in toks if self.waited[eng].get(s, 0) < v]
        for s, v in waits:
            self.waited[eng][s] = v
        self.ops[eng].append((waits, None, None, 0))

    def replay(self, e, eng):
        for waits, fn, tok, inc in self.ops[eng]:
            for s, v in waits:
                e.wait_ge(self.sems[s], v)
            if fn is None:
                continue
            ins = fn(e)
            if tok is not None:
                ins.then_inc(self.sems[tok[0]], inc)


def _split(n, maxn=512):
    k = -(-n // maxn)
    while n % k or (n // k) % 2:
        k += 1
    return k, n // k


def build_program():
    nc = bass.Bass("TRN2", target_bir_lowering=False)
    es = ExitStack()
    with es:
        def din(name, shape):
            return nc.dram_tensor(name, list(shape), F32, kind="ExternalInput").ap()

        xs = din("xs", [TOK_IN, D])
        cvec_d = din("cvec", [128, 8])
        mask_d = din("mask", [128, 1])
        vecs_d = din("vecs", [128, DEPTH, NV])
        rows_d = din("rows", [DEPTH, 128, 2, D])
        bsb_d = din("bsb", [DEPTH, 128, 8, 128])
        wsT_d = din("wsT", [128, DEPTH * 8 * 128])
        bandg_d = din("bandg", [128, 4 * 2 * 128])
        bandf_d = din("bandf", [128, 4 * 2 * 128])
        ident_d = din("ident", [128, 128])
        w_ada = din("w_ada", [DEPTH, D, 6 * D])
        w_in = din("w_in", [DEPTH, D, DIN])
        w_a_out = din("w_a_out", [DEPTH, D, D])
        w_b_out = din("w_b_out", [DEPTH, D, D])
        w_pool = din("w_pool", [DEPTH, 4, 256, 256])
        w_o = din("w_o", [DEPTH, D, D])
        w_up = din("w_up", [DEPTH, D, 2 * DFF])
        w_down = din("w_down", [DEPTH, DFF, D])
        y = nc.dram_tensor("y", [TOK_OUT, D], F32, kind="ExternalOutput").ap()

        def sb(name, shape, dt):
            return es.enter_context(nc.sbuf_tensor(name, list(shape), dt))

        def ps(name, shape):
            return es.enter_context(nc.psum_tensor(name, list(shape), F32))

        xT = sb("xT", [128, 8, TMAX], F32)
        hT = sb("hT", [128, 8, EMAX], BF16)
        preT = sb("preT", [128, 8, EMAX], BF16)
        ubuf = sb("ubuf", [128, 2, EMAX], BF16)
        BIG_MRG = 8 * EMAX * 4
        BIG_BYTES = BIG_MRG + 9 * 1024 * 2
        big = sb("big", [128, BIG_BYTES // 2], BF16)
        tmp = sb("tmp", [128, 3, EMAX], F32)
        gv = tmp[:, 2, 0:1024]
        wsl = sb("wsl", [128, NSLOT, SLOT_ELEMS], BF16)
        bc = sb("bc", [128, 2, D], F32)
        bsb = sb("bsb_s", [128, 8, 128], F32)
        vecs = sb("vecs_s", [128, DEPTH, NV], F32)
        ada = sb("ada", [128, DEPTH, 48], F32)
        der = sb("der", [128, DEPTH, 8, 8], F32)
        cvec = sb("cvec_s", [128, 8], F32)
        cact = sb("cact", [128, 8], BF16)
        mask = sb("mask_s", [128, 1], F32)
        wsT = sb("wsT_s", [128, DEPTH, 8, 128], BF16)
        bandg = sb("bandg_s", [128, 4, 2, 128], BF16)
        bandf = sb("bandf_s", [128, 4, 2, 128], BF16)
        ident = sb("ident_s", [128, 128], F32)
        ones = sb("ones_s", [128, 128], BF16)
        pcar = sb("pcar", [128, DEPTH, D], BF16)
        hcar = sb("hcar", [128, DEPTH, 2, 8, LEAD], BF16)
        stat = sb("stat", [128, 32], F32)
        sc8 = sb("sc8", [128, 16], F32)
        epsc = sb("epsc", [128, 1], F32)
        xin = tmp[:, 0:2, 0:D]
        ost = tmp[:, 0:2, 0:D]

        PA = ps("PA", [128, 3, 512])
        PB = ps("PB", [128, 3, 512])
        PC = ps("PC", [128, 512])
        PD = ps("PD", [128, 512])

        mrg = big[:, 0:BIG_MRG // 2].bitcast(F32).rearrange("p (c t) -> p c t", c=8)
        tok = big[:, BIG_MRG // 2:BIG_BYTES // 2].rearrange("p (b f) -> p b f", b=9)
        fT = big[:, 0:NFC * EMAX].rearrange("p (c t) -> p c t", c=NFC)
        mbf = big[:, BIG_MRG // 2:BIG_BYTES // 2].rearrange("p (c t) -> p c t", c=8)

        R = Rec(nc, es)
        for i in range(NSLOT):
            R.newsem("w%d" % i)
        for n in ["cst", "cstp", "xin0", "xin1", "ost0", "ost1", "bc", "bsb"]:
            R.newsem(n)

        wstate = {"n": 0, "pinned": set(), "last": None}

        def wload(pieces, pin=False):
            while True:
                i = wstate["n"] % NSLOT
                wstate["n"] += 1
                if i not in wstate["pinned"]:
                    break
            if pin:
                wstate["pinned"].add(i)
            wstate["last"] = i
            slot = wsl[:, i, :]
            items = []
            for dstf, src in pieces:
                dst = dstf(slot)
                items.append((lambda e, dst=dst, src=src: e.dma_start(out=dst, in_=src), (), (dst,)))
            R.dma_group("pool", "w%d" % i, items)
            return slot

        def wgrp_k8(src2d, c0, ncols, pin=False):
            src = src2d.rearrange("(kc p) n -> p kc n", p=128)[:, :, c0:c0 + ncols]
            return wload([(lambda s: s[:, 0:8 * ncols].rearrange("p (k n) -> p k n", k=8), src)], pin=pin)

        def pinned_k8(src2d, c0, ncols):
            sl = wgrp_k8(src2d, c0, ncols, pin=True)
            return sl.rearrange("p (k n) -> p k n", k=8), wstate["last"]

        def unpin(*idx):
            for i in idx:
                wstate["pinned"].discard(i)

        pslot = {"n": 0, "m": 0}

        def big_ps():
            pslot["n"] += 1
            return PA if pslot["n"] % 2 else PB

        def small_ps():
            pslot["m"] += 1
            return PC if pslot["m"] % 2 else PD

        def mm(o, l, r, start, stop, signal):
            R.op("pe", lambda e: e.matmul(o, l, r, start=start, stop=stop), reads=(l, r), writes=(o,), signal=signal)

        def fm_proj(lhs_list, rhs_fn, geo, P=None):
            off, nseg, seglen = geo
            if P is None:
                P = big_ps()
            nk = len(lhs_list)
            for j in range(nseg):
                lo = off + j * seglen
                o = P[:, j, 0:seglen]
                for k in range(nk):
                    mm(o, lhs_list[k], rhs_fn(k, lo, lo + seglen), k == 0, k == nk - 1,
                       (j == nseg - 1) and (k == nk - 1))
            return P[:, 0:nseg, 0:seglen]

        def V(l, col, n=1):
            return vecs[:, l, col:col + n]

        def act_op(out, in_, func, bias=0.0, scale=1.0):
            rd = [in_] + [a for a in (bias, scale) if not isinstance(a, float)]
            R.op("act", lambda e: e.activation(out=out, in_=in_, func=func, bias=bias, scale=scale),
                 reads=rd, writes=(out,))

        def tt(out, in0, in1, op, eng="dve"):
            R.op(eng, lambda e: e.tensor_tensor(out=out, in0=in0, in1=in1, op=op), reads=(in0, in1), writes=(out,))

        def ts(out, in0, s1, op0, s2=None, op1=None, eng="dve"):
            rd = [in0] + [a for a in (s1, s2) if a is not None and not isinstance(a, float)]
            if op1 is None:
                R.op(eng, lambda e: e.tensor_scalar(out=out, in0=in0, scalar1=s1, scalar2=None, op0=op0),
                     reads=rd, writes=(out,))
            else:
                R.op(eng, lambda e: e.tensor_scalar(out=out, in0=in0, scalar1=s1, scalar2=s2, op0=op0, op1=op1),
                     reads=rd, writes=(out,))

        def stt(out, in0, scalar, in1, op0, op1, eng="dve"):
            rd = [in0, in1] + ([] if isinstance(scalar, float) else [scalar])
            R.op(eng, lambda e: e.scalar_tensor_tensor(out=out, in0=in0, scalar=scalar, in1=in1, op0=op0, op1=op1),
                 reads=rd, writes=(out,))

        def cp(out, in_, eng="dve"):
            R.op(eng, lambda e: e.tensor_copy(out=out, in_=in_), reads=(in_,), writes=(out,))

        def mset(ap, val, eng="dve"):
            R.op(eng, lambda e: e.memset(ap, val), writes=(ap,))

        def dma(eng, out, in_, sem, reads=(), writes=()):
            R.op(eng, lambda e: e.dma_start(out=out, in_=in_), reads=reads, writes=writes, sem=sem, inc=16)

        def v3(ap2d, nseg, seglen):
            return ap2d.rearrange("p (s n) -> p s n", s=nseg)

        dma("sp", vecs[:], vecs_d, "cst", writes=(vecs[:],))
        dma("sp", cvec[:], cvec_d, "cst", writes=(cvec[:],))
        dma("sp", mask[:], mask_d, "cst", writes=(mask[:],))
        dma("sp", ident[:], ident_d, "cst", writes=(ident[:],))
        for l in range(DEPTH):
            dma("pool", wsT[:, l].rearrange("p b c -> p (b c)"), wsT_d[:, l * 1024:(l + 1) * 1024], "cstp", writes=(wsT[:, l],))
        dma("pool", bandg[:].rearrange("p a b c -> p (a b c)"), bandg_d, "cstp", writes=(bandg[:],))
        dma("pool", bandf[:].rearrange("p a b c -> p (a b c)"), bandf_d, "cstp", writes=(bandf[:],))
        for name in list(R.tr.keys()):
            for k, rec in R.tr[name].items():
                if rec[0] is not None and rec[0][0] in ("cst", "cstp"):
                    rec[0] = (rec[0][0], R.val[rec[0][0]])

        mset(ones[:], 1.0)
        mset(epsc[:], EPS)
        mset(hT[:], 0.0)
        mset(pcar[:], 0.0)
        mset(hcar[:], 0.0)
        mset(tmp[:], 0.0)
        for l in range(DEPTH):
            R.op("dve", lambda e, l=l: e.memset(wsT[64:128, l, :, 0:64], 0.0), writes=(wsT[:, l, :, :],))
        act_op(cact[:], cvec[:], AF.Silu)

        def ada_step(l, g):
            Pc = small_ps()
            slot = wgrp_k8(w_ada[l], g * 512, 512)
            wv = slot.rearrange("p (k n) -> p k n", k=8)
            for jj in range(4):
                for k in range(8):
                    mm(Pc[:, jj:jj + 1], wv[:, k, jj * 128:(jj + 1) * 128], cact[:, k:k + 1], k == 0, k == 7, k == 7)
            tt(ada[:, l, 4 * g:4 * g + 4], Pc[:, 0:4], V(l, V_BADA + 4 * g, 4), ALU.add)

        def der_part1(l):
            sh1, sc1 = ada[:, l, 0:8], ada[:, l, 8:16]
            t1 = sc8[:, 0:8]
            ts(t1, sc1, 1.0, ALU.add)
            if l == 0:
                ts(der[:, l, 0, :], t1, 1.0 / ALPHA, ALU.mult)
                cp(der[:, l, 1, :], sh1)
            else:
                tt(der[:, l, 0, :], t1, V(l - 1, V_LN2G, 8), ALU.mult)
                tt(der[:, l, 1, :], t1, V(l - 1, V_LN2B, 8), ALU.mult)
                tt(der[:, l, 1, :], der[:, l, 1, :], sh1, ALU.add)

        def der_part2(l):
            sh2, sc2 = ada[:, l, 24:32], ada[:, l, 32:40]
            t2 = sc8[:, 8:16]
            ts(t2, sc2, 1.0, ALU.add)
            tt(der[:, l, 2, :], t2, V(l, V_LN1G, 8), ALU.mult)
            tt(der[:, l, 3, :], t2, V(l, V_LN1B, 8), ALU.mult)
            tt(der[:, l, 3, :], der[:, l, 3, :], sh2, ALU.add)
            ts(der[:, l, 4, :], V(l, V_LN1G, 8), ALPHA, ALU.mult)
            ts(der[:, l, 5, :], V(l, V_LN1B, 8), ALPHA, ALU.mult)
            ts(der[:, l, 6, :], V(l, V_LN2G, 8), ALPHA, ALU.mult)
            ts(der[:, l, 7, :], V(l, V_LN2B, 8), ALPHA, ALU.mult)

        todo = []

        def tick(n=1):
            for _ in range(n):
                if todo:
                    todo.pop(0)()

        def drain():
            while todo:
                todo.pop(0)()

        def layer_norm_fm(c0, T, outs):
            nseg, seglen = _split(T)
            S1 = big_ps()
            S2 = big_ps()
            for c in range(8):
                r = xT[:, c, c0:c0 + T]
                rbc = ubuf[:, 0, 0:T]
                rsc = ubuf[:, 1, 0:T]
                cp(rbc, r)
                act_op(rsc, r, AF.Square)
                for j in range(nseg):
                    lo = j * seglen
                    mm(S1[:, j, 0:seglen], ones[:], rbc[:, lo:lo + seglen], c == 0, c == 7, False)
                    mm(S2[:, j, 0:seglen], ones[:], rsc[:, lo:lo + seglen], c == 0, c == 7, j == nseg - 1)
            mean = v3(tmp[:, 1, 0:T], nseg, seglen)
            rstd = v3(tmp[:, 2, 0:T], nseg, seglen)
            ts(mean, S1[:, 0:nseg, 0:seglen], 1.0 / D, ALU.mult)
            tt(rstd, mean, mean, ALU.mult)
            stt(rstd, S2[:, 0:nseg, 0:seglen], 1.0 / D, rstd, ALU.mult, ALU.subtract)
            act_op(rstd, rstd, AF.Sqrt, bias=epsc[:, 0:1])
            R.op("dve", lambda e: e.reciprocal(out=rstd, in_=rstd), reads=(rstd,), writes=(rstd,))
            meanf, rstdf = tmp[:, 1, 0:T], tmp[:, 2, 0:T]
            for c in range(8):
                r = xT[:, c, c0:c0 + T]
                yh = tmp[:, 0, 0:T] if c % 2 == 0 else ubuf[:].rearrange("p a b -> p (a b)").bitcast(F32)[:, 0:T]
                tt(yh, r, meanf, ALU.subtract)
                tt(yh, yh, rstdf, ALU.mult)
                for dst_fn, sc_fn, b_fn in outs:
                    act_op(dst_fn(c), yh, AF.Identity, bias=b_fn(c), scale=sc_fn(c))

        def geom(c0, T):
            for lead in (2, 4, 6):
                if _split(T + lead)[0] <= 3:
                    break
            else:
                raise AssertionError("no segment geometry for T=%d" % T)
            assert _split(T)[0] <= 3
            E = T + lead
            nse, sle = _split(E)
            ns, sl = _split(T)
            hcol0 = LEAD + c0
            return lead, E, nse, sle, ns, sl, hcol0

        def mixing(l, c0, T, gtile0, first_tile):
            tb0, off = c0 // 128, c0 % 128
            nb = 9 - tb0
            assert off + T == nb * 128
            tcol0 = LEAD + tb0 * 128
            lead, E, nse, sle, ns, sl, hcol0 = geom(c0, T)
            geoE = (hcol0 - lead, nse, sle)
            geoP = (0, nse, sle)
            geoT = (lead, ns, sl)
            cp(hT[:, :, 0:LEAD], hcar[:, l, 0, :, :])
            cp(hcar[:, l, 0, :, :], hT[:, :, LEAD + TMAX - LEAD:LEAD + TMAX])
            dma("sp", bc[:], rows_d[l], "bc", writes=(bc[:],))
            dma("sp", bsb[:], bsb_d[l], "bsb", writes=(bsb[:],))
            for hh in range(2):
                Pr = small_ps()
                mm(Pr[:], ones[:], wsT[:, l, 4 * hh:4 * hh + 4, :].rearrange("p a b -> p (a b)"), True, True, True)
                for gg in range(4):
                    g = 4 * hh + gg
                    stt(bsb[:, g, :], Pr[:, gg * 128:(gg + 1) * 128], V(l, V_LNVB + g), bsb[:, g, :], ALU.mult, ALU.add)
            nmask = (lead + max(0, 256 - (gtile0 * 128 + c0))) if first_tile else 0
            win_v = w_in[l].rearrange("(kc p) (r n) -> p kc r n", p=128, n=128)

            def hrhs(k, lo, hi):
                return hT[:, k, lo:hi]

            def prhs(k, lo, hi):
                return preT[:, k, lo:hi]

            def gated_out(wmat, gate_col0, mode):
                for g2 in range(2):
                    gslot, gi = pinned_k8(w_in[l], gate_col0 + g2 * 512, 512)
                    wslot, wi = pinned_k8(wmat, g2 * 512, 512)
                    for jj in range(4):
                        j = g2 * 4 + jj
                        Pg = fm_proj([gslot[:, k, jj * 128:(jj + 1) * 128] for k in range(8)], hrhs, geoE)
                        gate = v3(tmp[:, 0, 0:E], nse, sle)
                        act_op(gate, Pg, AF.Sigmoid, bias=V(l, V_BIN + gate_col0 // 128 + j))
                        Py = fm_proj([wslot[:, k, jj * 128:(jj + 1) * 128] for k in range(8)], prhs, geoP)
                        mj = v3(mrg[:, j, 0:E], nse, sle)
                        if mode == "set":
                            tt(mj, Py, gate, ALU.mult)
                        else:
                            t1 = v3(tmp[:, 1, 0:E], nse, sle)
                            tt(t1, Py, gate, ALU.mult)
                            tt(mj, mj, t1, ALU.add)
                        tick()
                    unpin(gi, wi)

            for c in range(8):
                slot = wload([
                    (lambda s, r=r: s[:, 0:3072].rearrange("p (k r n) -> p k r n", k=8, r=3)[:, :, r, :],
                     win_v[:, :, c + 8 * r, :]) for r in range(3)
                ])
                sv = slot[:, 0:3072].rearrange("p (k r n) -> p k r n", k=8, r=3)
                Pzx = fm_proj([sv[:, k, 2, :] for k in range(8)], hrhs, geoE)
                Pzc = fm_proj([sv[:, k, 1, :] for k in range(8)], hrhs, geoE)
                zx = v3(tmp[:, 0, 0:E], nse, sle)
                act_op(zx, Pzx, AF.Identity, bias=V(l, V_BIN + 16 + c))
                prodf = tmp[:, 1, 0:E]
                stt(v3(prodf, nse, sle), Pzc, V(l, V_BIN + 8 + c), zx, ALU.add, ALU.mult)
                if nmask:
                    ts(prodf[:, 0:nmask], prodf[:, 0:nmask], mask[:, 0:1], ALU.mult)
                Pzb = fm_proj([sv[:, k, 0, :] for k in range(8)], hrhs, geoE)
                cv = tmp[:, 2, 0:E]
                act_op(cv, prodf, AF.Identity, scale=V(l, V_CONVA + 16 + c))
                stt(cv[:, 2:E], prodf[:, 1:E - 1], V(l, V_CONVA + 8 + c), cv[:, 2:E], ALU.mult, ALU.add)
                stt(cv[:, 2:E], prodf[:, 0:E - 2], V(l, V_CONVA + 0 + c), cv[:, 2:E], ALU.mult, ALU.add)
                stt(v3(preT[:, c, 0:E], nse, sle), Pzb, V(l, V_BIN + c), v3(cv, nse, sle), ALU.add, ALU.mult)
                tick()
            gated_out(w_a_out[l], 6144, "set")

            zvslots = [wgrp_k8(w_in[l], 4096 + hf * 512, 512).rearrange("p (k n) -> p k n", k=8) for hf in range(2)]
            def v_s1(b):
                gv = tmp[:, 1 + b % 2, 0:1024]
                st = stat[:, (b % 2) * 16:(b % 2) * 16 + 16]
                for hf in range(2):
                    Pt = small_ps()
                    for k in range(8):
                        mm(Pt[:], hT[:, k, tcol0 + b * 128:tcol0 + (b + 1) * 128], zvslots[hf][:, k, :], k == 0, k == 7, k == 7)
                    gsl = gv[:, hf * 512:(hf + 1) * 512]
                    tt(gsl, Pt[:], bc[:, 0, hf * 512:(hf + 1) * 512], ALU.add)
                    act_op(gsl, gsl, AF.Gelu_apprx_tanh)
                for hf in range(2):
                    gsl = gv[:, hf * 512:(hf + 1) * 512]
                    R.op("dve", lambda e, hf=hf, gsl=gsl, st=st: e.bn_stats(out=st[:, hf * 6:(hf + 1) * 6], in_=gsl),
                         reads=(gsl,), writes=(st[:, hf * 6:(hf + 1) * 6],))

            def v_s2(b):
                gv = tmp[:, 1 + b % 2, 0:1024]
                st = stat[:, (b % 2) * 16:(b % 2) * 16 + 16]
                R.op("dve", lambda e: e.bn_aggr(out=st[:, 12:14], in_=st[:, 0:12].rearrange("p (a b) -> p a b", a=2)),
                     reads=(st[:, 0:12],), writes=(st[:, 12:14],))
                act_op(st[:, 14:15], st[:, 13:14], AF.Sqrt, bias=epsc[:, 0:1])
                R.op("dve", lambda e: e.reciprocal(out=st[:, 14:15], in_=st[:, 14:15]),
                     reads=(st[:, 14:15],), writes=(st[:, 14:15],))
                ts(tok[:, b, :], gv, st[:, 12:13], ALU.subtract, st[:, 14:15], ALU.mult)

            v_s1(0)
            for b in range(1, nb):
                v_s1(b)
                v_s2(b - 1)
            v_s2(nb - 1)
            for g2 in range(2):
                uslot, ui = pinned_k8(w_in[l], 3072 + g2 * 512, 512)
                for jj in range(4):
                    g = g2 * 4 + jj
                    Pu = fm_proj([uslot[:, k, jj * 128:(jj + 1) * 128] for k in range(8)], hrhs, geoE)
                    u = ubuf[:, g % 2, 0:E]
                    act_op(v3(u, nse, sle), Pu, AF.Gelu_apprx_tanh, bias=V(l, V_BIN + 24 + g))
                    Pm = big_ps()
                    Pmf = Pm[:].rearrange("p a b -> p (a b)")
                    for b in range(nb):
                        mm(Pmf[:, b * 128:(b + 1) * 128], tok[:, b, g * 128:(g + 1) * 128], wsT[:, l, g, :],
                           True, True, b == nb - 1)
                    mx = tmp[:, 0, 0:nb * 128]
                    stt(mx.rearrange("p (b i) -> p b i", b=nb), Pmf[:, 0:nb * 128].rearrange("p (b i) -> p b i", b=nb),
                        V(l, V_LNVG + g), bsb[:, g:g + 1, :].broadcast_to([128, nb, 128]), ALU.mult, ALU.add)
                    mset(preT[:, g, 0:lead], 0.0)
                    tt(preT[:, g, lead:E], mx[:, off:off + T], u[:, lead:E], ALU.mult)
                    tick()
                unpin(ui)
            gated_out(w_b_out[l], 7168, "add")

            zpslots = [wgrp_k8(w_in[l], 5120 + hf * 512, 512).rearrange("p (k n) -> p k n", k=8) for hf in range(2)]
            for b in range(nb):
                for hf in range(2):
                    Pt = small_ps()
                    for k in range(8):
                        mm(Pt[:], hT[:, k, tcol0 + b * 128:tcol0 + (b + 1) * 128], zpslots[hf][:, k, :], k == 0, k == 7, k == 7)
                    tt(tok[:, b, hf * 512:(hf + 1) * 512], Pt[:], bc[:, 1, hf * 512:(hf + 1) * 512], ALU.add)
            for c in range(8):
                kw = c // 2
                Pm = big_ps()
                Pmf = Pm[:].rearrange("p a b -> p (a b)")
                for b in range(nb):
                    band = bandf if (gtile0 + tb0 + b) == 2 else bandg
                    prev = pcar[:, l, c * 128:(c + 1) * 128] if b == 0 else tok[:, b - 1, c * 128:(c + 1) * 128]
                    o = Pmf[:, b * 128:(b + 1) * 128]
                    mm(o, tok[:, b, c * 128:(c + 1) * 128], band[:, kw, 0, :], True, False, False)
                    mm(o, prev, band[:, kw, 1, :], False, True, b == nb - 1)
                act_op(preT[:, c, lead:E], Pmf[:, off:off + T], AF.Identity)
            cp(pcar[:, l, :], tok[:, nb - 1, :])
            pslot_w = wload([
                (lambda s: s[:, 0:2048].rearrange("p (wk n) -> p wk n", wk=8),
                 w_pool[l].rearrange("w (kc p) n -> p (w kc) n", p=128)),
            ])[:, 0:2048].rearrange("p (w k n) -> p w k n", w=4, k=2)
            for g2 in range(2):
                gslot = wgrp_k8(w_in[l], 8192 + g2 * 512, 512).rearrange("p (k n) -> p k n", k=8)
                for jj in range(4):
                    j = g2 * 4 + jj
                    kw = j // 2
                    Pg = fm_proj([gslot[:, k, jj * 128:(jj + 1) * 128] for k in range(8)], hrhs, geoE)
                    gate = v3(tmp[:, 0, 0:E], nse, sle)
                    act_op(gate, Pg, AF.Sigmoid, bias=V(l, V_BIN + 64 + j))
                    Py = fm_proj([pslot_w[:, kw, kc, (j % 2) * 128:(j % 2 + 1) * 128] for kc in range(2)],
                                 lambda k, lo, hi, kw=kw: preT[:, 2 * kw + k, lo:hi], geoP)
                    t1 = v3(tmp[:, 1, 0:E], nse, sle)
                    stt(t1, Py, V(l, V_PSC + j), gate, ALU.mult, ALU.mult)
                    tt(mbf[:, j, 0:T], mrg[:, j, lead:E], tmp[:, 1, lead:E], ALU.add)

            drain()
            for g2 in range(2):
                wslot = wgrp_k8(w_o[l], g2 * 512, 512).rearrange("p (k n) -> p k n", k=8)
                for jj in range(4):
                    j = g2 * 4 + jj
                    Po = fm_proj([wslot[:, k, jj * 128:(jj + 1) * 128] for k in range(8)],
                                 lambda k, lo, hi: mbf[:, k, lo:hi], (0, ns, sl))
                    xv = v3(xT[:, j, c0:c0 + T], ns, sl)
                    stt(xv, Po, ada[:, l, 16 + j:17 + j], xv, ALU.mult, ALU.add)
            layer_norm_fm(c0, T, [
                (lambda c: xT[:, c, c0:c0 + T], lambda c: der[:, l, 4, c:c + 1], lambda c: der[:, l, 5, c:c + 1]),
                (lambda c: hT[:, c, hcol0:hcol0 + T], lambda c: der[:, l, 2, c:c + 1], lambda c: der[:, l, 3, c:c + 1]),
            ])

        def ffn(l, c0, T, gtile0, first_tile):
            lead, E, nse, sle, ns, sl, hcol0 = geom(c0, T)
            geoE = (hcol0 - lead, nse, sle)
            cp(hT[:, :, 0:LEAD], hcar[:, l, 1, :, :])
            cp(hcar[:, l, 1, :, :], hT[:, :, LEAD + TMAX - LEAD:LEAD + TMAX])
            nmask = (lead + max(0, 256 - (gtile0 * 128 + c0))) if first_tile else 0
            wup_v = w_up[l].rearrange("(kc p) (r n) -> p kc r n", p=128, r=2)

            def hrhs(k, lo, hi):
                return hT[:, k, lo:hi]

            for i in range(NFC // 2):
                slot = wload([
                    (lambda s, r=r: s[:, 0:4096].rearrange("p (k r n) -> p k r n", k=8, r=2)[:, :, r, :],
                     wup_v[:, :, r, i * 256:(i + 1) * 256]) for r in range(2)
                ])
                sv = slot[:, 0:4096].rearrange("p (k r n) -> p k r n", k=8, r=2)
                for jj in range(2):
                    c = 2 * i + jj
                    Pa = fm_proj([sv[:, k, 0, jj * 128:(jj + 1) * 128] for k in range(8)], hrhs, geoE)
                    af = tmp[:, 0, 0:E]
                    act_op(v3(af, nse, sle), Pa, AF.Identity, bias=V(l, V_BUP + c))
                    if nmask:
                        ts(af[:, 0:nmask], af[:, 0:nmask], mask[:, 0:1], ALU.mult)
                    cv = tmp[:, 1, 0:E]
                    act_op(cv, af, AF.Identity, scale=V(l, V_CONVF + 44 + c))
                    Pg = fm_proj([sv[:, k, 1, jj * 128:(jj + 1) * 128] for k in range(8)], hrhs, geoE)
                    gsb = ubuf[:].rearrange("p a b -> p (a b)").bitcast(F32)[:, 0:E]
                    act_op(v3(gsb, nse, sle), Pg, AF.Identity, bias=V(l, V_BUP + NFC + c))
                    stt(cv[:, 2:E], af[:, 1:E - 1], V(l, V_CONVF + 22 + c), cv[:, 2:E], ALU.mult, ALU.add)
                    stt(cv[:, 2:E], af[:, 0:E - 2], V(l, V_CONVF + 0 + c), cv[:, 2:E], ALU.mult, ALU.add)
                    gl = tmp[:, 2, 0:E]
                    act_op(gl[:, 2:E], cv[:, 2:E], AF.Gelu_apprx_tanh, bias=V(l, V_CONVFB + c))
                    tt(fT[:, c, 0:E], gsb, gl, ALU.mult)
            wd_v = w_down[l].rearrange("(kc p) n -> p kc n", p=128)
            for jp in range(4):
                Ps = [big_ps(), big_ps()]
                for kh in range(2):
                    slot = wload([
                        (lambda s: s[:, 0:2816].rearrange("p (k n) -> p k n", k=11),
                         wd_v[:, kh * 11:(kh + 1) * 11, jp * 256:(jp + 1) * 256]),
                    ])
                    sv = slot[:, 0:2816].rearrange("p (k n) -> p k n", k=11)
                    for jj in range(2):
                        for s in range(ns):
                            o = Ps[jj][:, s, 0:sl]
                            for k in range(11):
                                mm(o, sv[:, k, jj * 128:(jj + 1) * 128],
                                   fT[:, kh * 11 + k, lead + s * sl:lead + (s + 1) * sl],
                                   kh == 0 and k == 0, kh == 1 and k == 10, s == ns - 1 and k == 10)
                for jj in range(2):
                    j = jp * 2 + jj
                    xv = v3(xT[:, j, c0:c0 + T], ns, sl)
                    stt(xv, Ps[jj][:, 0:ns, 0:sl], ada[:, l, 40 + j:41 + j], xv, ALU.mult, ALU.add)
            if l < DEPTH - 1:
                layer_norm_fm(c0, T, [
                    (lambda c: xT[:, c, c0:c0 + T], lambda c: der[:, l, 6, c:c + 1], lambda c: der[:, l, 7, c:c + 1]),
                    (lambda c: hT[:, c, hcol0:hcol0 + T], lambda c: der[:, l + 1, 0, c:c + 1], lambda c: der[:, l + 1, 1, c:c + 1]),
                ])
            else:
                layer_norm_fm(c0, T, [
                    (lambda c: xT[:, c, c0:c0 + T], lambda c: V(l, V_LN2G + c), lambda c: V(l, V_LN2B + c)),
                ])

        iost = {"i": 0, "o": 0}

        def load_x(gblk_list):
            for bi, gb in enumerate(gblk_list):
                s = iost["i"] % 2
                iost["i"] += 1
                dma("sp", xin[:, s, :], xs[gb * 128:(gb + 1) * 128, :], "xin%d" % s, writes=(xin[:, s, :],))
                for half, P in enumerate((PC, PD)):
                    for cc in range(4):
                        c = half * 4 + cc
                        o = P[:, cc * 128:(cc + 1) * 128]
                        i_ap = xin[:, s, c * 128:(c + 1) * 128]
                        R.op("pe", lambda e, o=o, i_ap=i_ap: e.transpose(o, i_ap, ident[:]),
                             reads=(i_ap, ident[:]), writes=(o,), signal=(cc == 3))
                    ts(xT[:, half * 4:half * 4 + 4, bi * 128:(bi + 1) * 128],
                       P[:, 0:512].rearrange("p (c t) -> p c t", c=4), ALPHA, ALU.mult)

        def make_h0(T):
            for c in range(8):
                act_op(hT[:, c, LEAD:LEAD + T], xT[:, c, 0:T], AF.Identity,
                       bias=der[:, 0, 1, c:c + 1], scale=der[:, 0, 0, c:c + 1])

        def store_y(col_blocks):
            for b, ob in col_blocks:
                s = iost["o"] % 2
                iost["o"] += 1
                for half, P in enumerate((PC, PD)):
                    for cc in range(4):
                        c = half * 4 + cc
                        o = P[:, cc * 128:(cc + 1) * 128]
                        i_ap = xT[:, c, b * 128:(b + 1) * 128]
                        R.op("pe", lambda e, o=o, i_ap=i_ap: e.transpose(o, i_ap, ident[:]),
                             reads=(i_ap, ident[:]), writes=(o,), signal=(cc == 3))
                    if half == 0:
                        act_op(ost[:, s, 0:512], P[:, 0:512], AF.Identity)
                    else:
                        cp(ost[:, s, 512:1024], P[:, 0:512])
                dma("sp", y[ob * 128:(ob + 1) * 128, :], ost[:, s, :], "ost%d" % s, reads=(ost[:, s, :],))

        load_x(list(range(0, 9)))
        for g in range(4):
            ada_step(0, g)
        der_part1(0)
        make_h0(1152)
        for g in range(4, 12):
            todo.append(lambda g=g: ada_step(0, g))
        todo.append(lambda: der_part2(0))
        for g in range(12):
            todo.append(lambda g=g: ada_step(1, g))
        todo.append(lambda: der_part1(1))
        todo.append(lambda: der_part2(1))
        mixing(0, 120, 1032, 0, True)
        ffn(0, 120, 1032, 0, True)
        mixing(1, 248, 904, 0, True)
        ffn(1, 248, 904, 0, True)
        store_y([(b, b - 2) for b in range(2, 9)])
        load_x(list(range(9, 18)))
        make_h0(1152)
        mixing(0, 0, 1152, 9, False)
        ffn(0, 0, 1152, 9, False)
        mixing(1, 0, 1152, 9, False)
        ffn(1, 0, 1152, 9, False)
        store_y([(b, 7 + b) for b in range(0, 9)])
        R.wait_all("sp", [("ost0", R.val["ost0"]), ("ost1", R.val["ost1"])])

        with nc.Block() as block:
            @block.tensor
            def _(e):
                R.replay(e, "pe")

            @block.scalar
            def _(e):
                R.replay(e, "act")

            @block.vector
            def _(e):
                R.replay(e, "dve")

            @block.gpsimd
            def _(e):
                R.replay(e, "pool")

            @block.sync
            def _(e):
                R.replay(e, "sp")
    return nc


def _fm(v, nch):
    return np.ascontiguousarray(np.asarray(v, np.float32).reshape(nch, 128).T)


def _bands(first):
    out = np.zeros((128, 4, 2, 128), np.float32)
    for k, w in enumerate(POOL_W):
        for t in range(128):
            if first:
                n = min(t + 1, w)
                for tp in range(max(0, t - w + 1), t + 1):
                    out[tp, k, 0, t] += 1.0 / n
            else:
                for tp in range(t - w + 1, t + 1):
                    if tp >= 0:
                        out[tp, k, 0, t] += 1.0 / w
                    else:
                        out[tp + 128, k, 1, t] += 1.0 / w
            out[t, k, 0, t] -= 1.0
    return out.reshape(128, 4 * 2 * 128)


_CACHE = {}


def kernel(x, c, w_ada, b_ada, w_in, b_in, conv_a, w_a_out, ln_v_g, ln_v_b,
           w_spatial, b_spatial, w_b_out, w_pool, pool_scale, w_o, ln1_g, ln1_b,
           w_up, b_up, conv_ffn, conv_ffn_b, w_down, ln2_g, ln2_b):
    f = lambda a: np.ascontiguousarray(np.asarray(a, dtype=np.float32))
    x = f(x)
    c = f(c)
    vecs = np.zeros((128, DEPTH, NV), np.float32)
    rows = np.zeros((DEPTH, 128, 2, D), np.float32)
    bsb = np.zeros((DEPTH, 128, 8, 128), np.float32)
    wsT = np.zeros((128, DEPTH, 8, 128), np.float32)
    for l in range(DEPTH):
        vecs[:, l, V_BIN:V_BIN + 72] = _fm(b_in[l], 72)
        vecs[:, l, V_BADA:V_BADA + 48] = _fm(b_ada[l], 48)
        vecs[:, l, V_CONVA:V_CONVA + 24] = np.concatenate([_fm(conv_a[l][k], 8) for k in range(3)], axis=1)
        vecs[:, l, V_LN1G:V_LN1G + 8] = _fm(ln1_g[l], 8)
        vecs[:, l, V_LN1B:V_LN1B + 8] = _fm(ln1_b[l], 8)
        vecs[:, l, V_LN2G:V_LN2G + 8] = _fm(ln2_g[l], 8)
        vecs[:, l, V_LN2B:V_LN2B + 8] = _fm(ln2_b[l], 8)
        vecs[:, l, V_PSC:V_PSC + 8] = _fm(pool_scale[l], 8)
        vecs[:, l, V_BUP:V_BUP + 44] = _fm(b_up[l], 44)
        vecs[:, l, V_CONVF:V_CONVF + 66] = np.concatenate([_fm(conv_ffn[l][k], NFC) for k in range(3)], axis=1)
        vecs[:, l, V_CONVFB:V_CONVFB + NFC] = _fm(conv_ffn_b[l], NFC)
        vecs[:, l, V_LNVG:V_LNVG + 8] = _fm(ln_v_g[l], 8)
        vecs[:, l, V_LNVB:V_LNVB + 8] = _fm(ln_v_b[l], 8)
        bl = np.asarray(b_in[l], np.float32)
        rows[l, :, 0, :] = bl[4096:5120][None, :]
        rows[l, :, 1, :] = bl[5120:6144][None, :]
        bsb[l] = np.asarray(b_spatial[l], np.float32)[None, :, :]
        wsT[:, l] = np.transpose(np.asarray(w_spatial[l], np.float32), (2, 0, 1))
    wsT = np.ascontiguousarray(wsT.reshape(128, DEPTH * 8 * 128))
    bandg = _bands(False)
    bandf = _bands(True)
    ident = np.eye(128, dtype=np.float32)
    shared = {
        "vecs": vecs, "rows": rows, "bsb": bsb, "wsT": wsT, "bandg": bandg, "ident": ident,
        "w_ada": f(w_ada), "w_in": f(w_in), "w_a_out": f(w_a_out), "w_b_out": f(w_b_out),
        "w_pool": f(w_pool), "w_o": f(w_o), "w_up": f(w_up), "w_down": f(w_down),
    }
    in_maps = []
    for core in range(8):
        b, half = core // 2, core % 2
        xs = np.zeros((TOK_IN, D), np.float32)
        if half == 0:
            xs[256:] = x[b, 0:2048]
        else:
            xs[:] = x[b, 2048 - 256:4096]
        m = dict(shared)
        m["xs"] = xs
        m["cvec"] = _fm(c[b], 8)
        m["mask"] = np.full((128, 1), float(half), np.float32)
        m["bandf"] = bandf if half == 0 else bandg
        in_maps.append(m)
    if "nc" not in _CACHE:
        _CACHE["nc"] = build_program()
    res = run_bass_kernel_spmd(_CACHE["nc"], in_maps, core_ids=list(range(8)))
    out = np.zeros((NBATCH, SEQ, D), np.float32)
    for core in range(8):
        b, half = core // 2, core % 2
        out[b, half * 2048:(half + 1) * 2048] = res.results[core]["y"]
    return out
```

```python
import numpy as np
from contextlib import ExitStack
import concourse.bass as bass
import concourse.mybir as mybir
from concourse.bass_utils import run_bass_kernel_spmd

F32 = mybir.dt.float32
BF16 = mybir.dt.bfloat16
AF = mybir.ActivationFunctionType
ALU = mybir.AluOpType
ESZ = {F32: 4, BF16: 2}
PSUM_NAMES = ("PA", "PB", "PC", "PD")

D = 1024
SEQ = 4096
NBATCH = 4
DEPTH = 2
DFF = 2816
NFC = DFF // 128
DIN = 9216
ALPHA = (2 * DEPTH) ** 0.25
EPS = 1e-5
POOL_W = (2, 4, 8, 16)
NBLK_IN = 18
TOK_IN = NBLK_IN * 128
TOK_OUT = 2048
TMAX = 1152
LEAD = 6
EMAX = TMAX + LEAD
NSLOT = 4
SLOT_ELEMS = 4096

V_BIN = 0
V_BADA = 72
V_CONVA = 120
V_LN1G = 144
V_LN1B = 152
V_LN2G = 160
V_LN2B = 168
V_PSC = 176
V_BUP = 184
V_CONVF = 228
V_CONVFB = 294
V_LNVG = 316
V_LNVB = 324
NV = 332


def _rng(ap):
    a = ap.ap
    esz = ESZ[ap.dtype]
    pstep = a[0][0]
    off = ap.offset % pstep if pstep > 0 else ap.offset
    span = 1
    for step, cnt in a[1:]:
        span += (cnt - 1) * abs(step)
    lo, hi = off * esz, (off + span) * esz
    if ap.tensor.name in PSUM_NAMES:
        lo = (lo // 2048) * 2048
        hi = -(-hi // 2048) * 2048
    return ap.tensor.name, lo, hi


class Rec:
    def __init__(self, nc, es):
        self.nc = nc
        self.es = es
        self.engs = ["pe", "act", "dve", "pool", "sp"]
        self.sems = {}
        self.val = {}
        self.waited = {e: {} for e in self.engs}
        self.ops = {e: [] for e in self.engs}
        self.tr = {}
        self.pending = {e: ([], []) for e in self.engs}
        for e in self.engs:
            self.newsem(e)

    def newsem(self, name):
        self.sems[name] = self.es.enter_context(self.nc.semaphore("s_" + name))
        self.val[name] = 0

    def _need(self, reads, writes):
        need = {}

        def add(tok):
            if tok is None:
                return
            s, v = tok
            if need.get(s, 0) < v:
                need[s] = v

        for ap in reads:
            name, lo, hi = _rng(ap)
            for (l2, h2), rec in self.tr.get(name, {}).items():
                if l2 < hi and lo < h2:
                    add(rec[0])
        for ap in writes:
            name, lo, hi = _rng(ap)
            for (l2, h2), rec in self.tr.get(name, {}).items():
                if l2 < hi and lo < h2:
                    add(rec[0])
                    for s, v in rec[1].items():
                        add((s, v))
        return need

    def _record(self, reads, writes, tok):
        for ap in reads:
            name, lo, hi = _rng(ap)
            d = self.tr.setdefault(name, {})
            rec = d.setdefault((lo, hi), [None, {}])
            if rec[1].get(tok[0], 0) < tok[1]:
                rec[1][tok[0]] = tok[1]
        for ap in writes:
            name, lo, hi = _rng(ap)
            d = self.tr.setdefault(name, {})
            for k in [k for k in d if lo <= k[0] and k[1] <= hi]:
                del d[k]
            d[(lo, hi)] = [tok, {}]

    def op(self, eng, fn, reads=(), writes=(), sem=None, inc=1, signal=True, rec_val=None):
        need = self._need(reads, writes)
        waits = []
        for s, v in need.items():
            if s == "pe" and eng == "pe":
                continue
            if self.waited[eng].get(s, 0) >= v:
                continue
            self.waited[eng][s] = v
            waits.append((s, v))
        tok = None
        if sem is not None:
            self.val[sem] += inc
            tok = (sem, self.val[sem])
            self._record(reads, writes, (sem, rec_val) if rec_val is not None else tok)
        else:
            pr, pw = self.pending[eng]
            pr.extend(reads)
            pw.extend(writes)
            if signal:
                self.val[eng] += inc
                tok = (eng, self.val[eng])
                self._record(pr, pw, tok)
                self.pending[eng] = ([], [])
        self.ops[eng].append((waits, fn, tok, inc))

    def dma_group(self, eng, sem, items):
        allr = [a for _, r, _ in items for a in r]
        allw = [a for _, _, w in items for a in w]
        need = self._need(allr, allw)
        waits = []
        for s, v in need.items():
            if self.waited[eng].get(s, 0) >= v:
                continue
            self.waited[eng][s] = v
            waits.append((s, v))
        final = self.val[sem] + 16 * len(items)
        for i, (fn, r, w) in enumerate(items):
            self.val[sem] += 16
            self.ops[eng].append((waits if i == 0 else [], fn, (sem, self.val[sem]), 16))
        self._record(allr, allw, (sem, final))

    def wait_all(self, eng, toks):
        waits = [(s, v) for s, v in toks if self.waited[eng].get(s, 0) < v]
        for s, v in waits:
            self.waited[eng][s] = v
        self.ops[eng].append((waits, None, None, 0))

    def replay(self, e, eng):
        for waits, fn, tok, inc in self.ops[eng]:
            for s, v in waits:
                e.wait_ge(self.sems[s], v)
            if fn is None:
                continue
            ins = fn(e)
            if tok is not None:
                ins.then_inc(self.sems[tok[0]], inc)


def _split(n, maxn=512):
    k = -(-n // maxn)
    while n % k or (n // k) % 2:
        k += 1
    return k, n // k


def build_program():
    nc = bass.Bass("TRN2", target_bir_lowering=False)
    es = ExitStack()
    with es:
        def din(name, shape):
            return nc.dram_tensor(name, list(shape), F32, kind="ExternalInput").ap()

        xs = din("xs", [TOK_IN, D])
        cvec_d = din("cvec", [128, 8])
        mask_d = din("mask", [128, 1])
        vecs_d = din("vecs", [128, DEPTH, NV])
        rows_d = din("rows", [DEPTH, 128, 2, D])
        bsb_d = din("bsb", [DEPTH, 128, 8, 128])
        wsT_d = din("wsT", [128, DEPTH * 8 * 128])
        bandg_d = din("bandg", [128, 4 * 2 * 128])
        bandf_d = din("bandf", [128, 4 * 2 * 128])
        ident_d = din("ident", [128, 128])
        w_ada = din("w_ada", [DEPTH, D, 6 * D])
        w_in = din("w_in", [DEPTH, D, DIN])
        w_a_out = din("w_a_out", [DEPTH, D, D])
        w_b_out = din("w_b_out", [DEPTH, D, D])
        w_pool = din("w_pool", [DEPTH, 4, 256, 256])
        w_o = din("w_o", [DEPTH, D, D])
        w_up = din("w_up", [DEPTH, D, 2 * DFF])
        w_down = din("w_down", [DEPTH, DFF, D])
        y = nc.dram_tensor("y", [TOK_OUT, D], F32, kind="ExternalOutput").ap()

        def sb(name, shape, dt):
            return es.enter_context(nc.sbuf_tensor(name, list(shape), dt))

        def ps(name, shape):
            return es.enter_context(nc.psum_tensor(name, list(shape), F32))

        xT = sb("xT", [128, 8, TMAX], F32)
        hT = sb("hT", [128, 8, EMAX], BF16)
        preT = sb("preT", [128, 8, EMAX], BF16)
        ubuf = sb("ubuf", [128, 2, EMAX], BF16)
        BIG_MRG = 8 * EMAX * 4
        BIG_BYTES = BIG_MRG + 9 * 1024 * 2
        big = sb("big", [128, BIG_BYTES // 2], BF16)
        tmp = sb("tmp", [128, 3, EMAX], F32)
        gv = tmp[:, 2, 0:1024]
        wsl = sb("wsl", [128, NSLOT, SLOT_ELEMS], BF16)
        bc = sb("bc", [128, 2, D], F32)
        bsb = sb("bsb_s", [128, 8, 128], F32)
        vecs = sb("vecs_s", [128, DEPTH, NV], F32)
        ada = sb("ada", [128, DEPTH, 48], F32)
        der = sb("der", [128, DEPTH, 8, 8], F32)
        cvec = sb("cvec_s", [128, 8], F32)
        cact = sb("cact", [128, 8], BF16)
        mask = sb("mask_s", [128, 1], F32)
        wsT = sb("wsT_s", [128, DEPTH, 8, 128], BF16)
        bandg = sb("bandg_s", [128, 4, 2, 128], BF16)
        bandf = sb("bandf_s", [128, 4, 2, 128], BF16)
        ident = sb("ident_s", [128, 128], F32)
        ones = sb("ones_s", [128, 128], BF16)
        pcar = sb("pcar", [128, DEPTH, D], BF16)
        hcar = sb("hcar", [128, DEPTH, 2, 8, LEAD], BF16)
        stat = sb("stat", [128, 32], F32)
        sc8 = sb("sc8", [128, 16], F32)
        epsc = sb("epsc", [128, 1], F32)
        xin = tmp[:, 0:2, 0:D]
        ost = tmp[:, 0:2, 0:D]

        PA = ps("PA", [128, 3, 512])
        PB = ps("PB", [128, 3, 512])
        PC = ps("PC", [128, 512])
        PD = ps("PD", [128, 512])

        mrg = big[:, 0:BIG_MRG // 2].bitcast(F32).rearrange("p (c t) -> p c t", c=8)
        tok = big[:, BIG_MRG // 2:BIG_BYTES // 2].rearrange("p (b f) -> p b f", b=9)
        fT = big[:, 0:NFC * EMAX].rearrange("p (c t) -> p c t", c=NFC)
        mbf = big[:, BIG_MRG // 2:BIG_BYTES // 2].rearrange("p (c t) -> p c t", c=8)

        R = Rec(nc, es)
        for i in range(NSLOT):
            R.newsem("w%d" % i)
        for n in ["cst", "cstp", "xin0", "xin1", "ost0", "ost1", "bc", "bsb"]:
            R.newsem(n)

        wstate = {"n": 0, "pinned": set(), "last": None}

        def wload(pieces, pin=False):
            while True:
                i = wstate["n"] % NSLOT
                wstate["n"] += 1
                if i not in wstate["pinned"]:
                    break
            if pin:
                wstate["pinned"].add(i)
            wstate["last"] = i
            slot = wsl[:, i, :]
            items = []
            for dstf, src in pieces:
                dst = dstf(slot)
                items.append((lambda e, dst=dst, src=src: e.dma_start(out=dst, in_=src), (), (dst,)))
            R.dma_group("pool", "w%d" % i, items)
            return slot

        def wgrp_k8(src2d, c0, ncols, pin=False):
            src = src2d.rearrange("(kc p) n -> p kc n", p=128)[:, :, c0:c0 + ncols]
            return wload([(lambda s: s[:, 0:8 * ncols].rearrange("p (k n) -> p k n", k=8), src)], pin=pin)

        def pinned_k8(src2d, c0, ncols):
            sl = wgrp_k8(src2d, c0, ncols, pin=True)
            return sl.rearrange("p (k n) -> p k n", k=8), wstate["last"]

        def unpin(*idx):
            for i in idx:
                wstate["pinned"].discard(i)

        pslot = {"n": 0, "m": 0}

        def big_ps():
            pslot["n"] += 1
            return PA if pslot["n"] % 2 else PB

        def small_ps():
            pslot["m"] += 1
            return PC if pslot["m"] % 2 else PD

        def mm(o, l, r, start, stop, signal):
            R.op("pe", lambda e: e.matmul(o, l, r, start=start, stop=stop), reads=(l, r), writes=(o,), signal=signal)

        def fm_proj(lhs_list, rhs_fn, geo, P=None):
            off, nseg, seglen = geo
            if P is None:
                P = big_ps()
            nk = len(lhs_list)
            for j in range(nseg):
                lo = off + j * seglen
                o = P[:, j, 0:seglen]
                for k in range(nk):
                    mm(o, lhs_list[k], rhs_fn(k, lo, lo + seglen), k == 0, k == nk - 1,
                       (j == nseg - 1) and (k == nk - 1))
            return P[:, 0:nseg, 0:seglen]

        def V(l, col, n=1):
            return vecs[:, l, col:col + n]

        def act_op(out, in_, func, bias=0.0, scale=1.0):
            rd = [in_] + [a for a in (bias, scale) if not isinstance(a, float)]
            R.op("act", lambda e: e.activation(out=out, in_=in_, func=func, bias=bias, scale=scale),
                 reads=rd, writes=(out,))

        def tt(out, in0, in1, op, eng="dve"):
            R.op(eng, lambda e: e.tensor_tensor(out=out, in0=in0, in1=in1, op=op), reads=(in0, in1), writes=(out,))

        def ts(out, in0, s1, op0, s2=None, op1=None, eng="dve"):
            rd = [in0] + [a for a in (s1, s2) if a is not None and not isinstance(a, float)]
            if op1 is None:
                R.op(eng, lambda e: e.tensor_scalar(out=out, in0=in0, scalar1=s1, scalar2=None, op0=op0),
                     reads=rd, writes=(out,))
            else:
                R.op(eng, lambda e: e.tensor_scalar(out=out, in0=in0, scalar1=s1, scalar2=s2, op0=op0, op1=op1),
                     reads=rd, writes=(out,))

        def stt(out, in0, scalar, in1, op0, op1, eng="dve"):
            rd = [in0, in1] + ([] if isinstance(scalar, float) else [scalar])
            R.op(eng, lambda e: e.scalar_tensor_tensor(out=out, in0=in0, scalar=scalar, in1=in1, op0=op0, op1=op1),
                 reads=rd, writes=(out,))

        def cp(out, in_, eng="dve"):
            R.op(eng, lambda e: e.tensor_copy(out=out, in_=in_), reads=(in_,), writes=(out,))

        def mset(ap, val, eng="dve"):
            R.op(eng, lambda e: e.memset(ap, val), writes=(ap,))

        def dma(eng, out, in_, sem, reads=(), writes=()):
            R.op(eng, lambda e: e.dma_start(out=out, in_=in_), reads=reads, writes=writes, sem=sem, inc=16)

        def v3(ap2d, nseg, seglen):
            return ap2d.rearrange("p (s n) -> p s n", s=nseg)

        dma("sp", vecs[:], vecs_d, "cst", writes=(vecs[:],))
        dma("sp", cvec[:], cvec_d, "cst", writes=(cvec[:],))
        dma("sp", mask[:], mask_d, "cst", writes=(mask[:],))
        dma("sp", ident[:], ident_d, "cst", writes=(ident[:],))
        for l in range(DEPTH):
            dma("pool", wsT[:, l].rearrange("p b c -> p (b c)"), wsT_d[:, l * 1024:(l + 1) * 1024], "cstp", writes=(wsT[:, l],))
        dma("pool", bandg[:].rearrange("p a b c -> p (a b c)"), bandg_d, "cstp", writes=(bandg[:],))
        dma("pool", bandf[:].rearrange("p a b c -> p (a b c)"), bandf_d, "cstp", writes=(bandf[:],))
        for name in list(R.tr.keys()):
            for k, rec in R.tr[name].items():
                if rec[0] is not None and rec[0][0] in ("cst", "cstp"):
                    rec[0] = (rec[0][0], R.val[rec[0][0]])

        mset(ones[:], 1.0)
        mset(epsc[:], EPS)
        mset(hT[:], 0.0)
        mset(pcar[:], 0.0)
        mset(hcar[:], 0.0)
        mset(tmp[:], 0.0)
        for l in range(DEPTH):
            R.op("dve", lambda e, l=l: e.memset(wsT[64:128, l, :, 0:64], 0.0), writes=(wsT[:, l, :, :],))
        act_op(cact[:], cvec[:], AF.Silu)

        def ada_step(l, g):
            Pc = small_ps()
            slot = wgrp_k8(w_ada[l], g * 512, 512)
            wv = slot.rearrange("p (k n) -> p k n", k=8)
            for jj in range(4):
                for k in range(8):
                    mm(Pc[:, jj:jj + 1], wv[:, k, jj * 128:(jj + 1) * 128], cact[:, k:k + 1], k == 0, k == 7, k == 7)
            tt(ada[:, l, 4 * g:4 * g + 4], Pc[:, 0:4], V(l, V_BADA + 4 * g, 4), ALU.add)

        def der_part1(l):
            sh1, sc1 = ada[:, l, 0:8], ada[:, l, 8:16]
            t1 = sc8[:, 0:8]
            ts(t1, sc1, 1.0, ALU.add)
            if l == 0:
                ts(der[:, l, 0, :], t1, 1.0 / ALPHA, ALU.mult)
                cp(der[:, l, 1, :], sh1)
            else:
                tt(der[:, l, 0, :], t1, V(l - 1, V_LN2G, 8), ALU.mult)
                tt(der[:, l, 1, :], t1, V(l - 1, V_LN2B, 8), ALU.mult)
                tt(der[:, l, 1, :], der[:, l, 1, :], sh1, ALU.add)

        def der_part2(l):
            sh2, sc2 = ada[:, l, 24:32], ada[:, l, 32:40]
            t2 = sc8[:, 8:16]
            ts(t2, sc2, 1.0, ALU.add)
            tt(der[:, l, 2, :], t2, V(l, V_LN1G, 8), ALU.mult)
            tt(der[:, l, 3, :], t2, V(l, V_LN1B, 8), ALU.mult)
            tt(der[:, l, 3, :], der[:, l, 3, :], sh2, ALU.add)
            ts(der[:, l, 4, :], V(l, V_LN1G, 8), ALPHA, ALU.mult)
            ts(der[:, l, 5, :], V(l, V_LN1B, 8), ALPHA, ALU.mult)
            ts(der[:, l, 6, :], V(l, V_LN2G, 8), ALPHA, ALU.mult)
            ts(der[:, l, 7, :], V(l, V_LN2B, 8), ALPHA, ALU.mult)

        todo = []

        def tick(n=1):
            for _ in range(n):
                if todo:
                    todo.pop(0)()

        def drain():
            while todo:
                todo.pop(0)()

        def ln_acc(j, c0, T):
            r = xT[:, j, c0:c0 + T]
            rsum, rsq = tmp[:, 1, 0:T], tmp[:, 2, 0:T]
            sqt = ubuf[:].rearrange("p a b -> p (a b)").bitcast(F32)[:, 0:T]
            if j == 0:
                act_op(rsum, r, AF.Identity)
                act_op(rsq, r, AF.Square)
            else:
                tt(rsum, rsum, r, ALU.add)
                act_op(sqt, r, AF.Square)
                tt(rsq, rsq, sqt, ALU.add)

        def layer_norm_fm(c0, T, outs):
            nseg, seglen = _split(T)
            rbc = ubuf[:, 0, 0:T]
            rsc = ubuf[:, 1, 0:T]
            cp(rbc, tmp[:, 1, 0:T])
            act_op(rsc, tmp[:, 2, 0:T], AF.Identity)
            S1 = big_ps()
            S2 = big_ps()
            for j in range(nseg):
                lo = j * seglen
                mm(S1[:, j, 0:seglen], ones[:], rbc[:, lo:lo + seglen], True, True, False)
                mm(S2[:, j, 0:seglen], ones[:], rsc[:, lo:lo + seglen], True, True, j == nseg - 1)
            mean = v3(tmp[:, 1, 0:T], nseg, seglen)
            rstd = v3(tmp[:, 2, 0:T], nseg, seglen)
            ts(mean, S1[:, 0:nseg, 0:seglen], 1.0 / D, ALU.mult)
            tt(rstd, mean, mean, ALU.mult)
            stt(rstd, S2[:, 0:nseg, 0:seglen], 1.0 / D, rstd, ALU.mult, ALU.subtract)
            act_op(rstd, rstd, AF.Sqrt, bias=epsc[:, 0:1])
            R.op("dve", lambda e: e.reciprocal(out=rstd, in_=rstd), reads=(rstd,), writes=(rstd,))
            meanf, rstdf = tmp[:, 1, 0:T], tmp[:, 2, 0:T]
            for c in range(8):
                r = xT[:, c, c0:c0 + T]
                yh = tmp[:, 0, 0:T] if c % 2 == 0 else ubuf[:].rearrange("p a b -> p (a b)").bitcast(F32)[:, 0:T]
                tt(yh, r, meanf, ALU.subtract)
                tt(yh, yh, rstdf, ALU.mult)
                for dst_fn, sc_fn, b_fn in outs:
                    act_op(dst_fn(c), yh, AF.Identity, bias=b_fn(c), scale=sc_fn(c))

        def geom(c0, T):
            for lead in (2, 4, 6):
                if _split(T + lead)[0] <= 3:
                    break
            else:
                raise AssertionError("no segment geometry for T=%d" % T)
            assert _split(T)[0] <= 3
            E = T + lead
            nse, sle = _split(E)
            ns, sl = _split(T)
            hcol0 = LEAD + c0
            return lead, E, nse, sle, ns, sl, hcol0

        def mixing(l, c0, T, gtile0, first_tile):
            tb0, off = c0 // 128, c0 % 128
            nb = 9 - tb0
            assert off + T == nb * 128
            tcol0 = LEAD + tb0 * 128
            lead, E, nse, sle, ns, sl, hcol0 = geom(c0, T)
            geoE = (hcol0 - lead, nse, sle)
            geoP = (0, nse, sle)
            geoT = (lead, ns, sl)
            cp(hT[:, :, 0:LEAD], hcar[:, l, 0, :, :])
            cp(hcar[:, l, 0, :, :], hT[:, :, LEAD + TMAX - LEAD:LEAD + TMAX])
            dma("sp", bc[:], rows_d[l], "bc", writes=(bc[:],))
            dma("sp", bsb[:], bsb_d[l], "bsb", writes=(bsb[:],))
            for hh in range(2):
                Pr = small_ps()
                mm(Pr[:], ones[:], wsT[:, l, 4 * hh:4 * hh + 4, :].rearrange("p a b -> p (a b)"), True, True, True)
                for gg in range(4):
                    g = 4 * hh + gg
                    stt(bsb[:, g, :], Pr[:, gg * 128:(gg + 1) * 128], V(l, V_LNVB + g), bsb[:, g, :], ALU.mult, ALU.add)
            nmask = (lead + max(0, 256 - (gtile0 * 128 + c0))) if first_tile else 0
            win_v = w_in[l].rearrange("(kc p) (r n) -> p kc r n", p=128, n=128)

            def hrhs(k, lo, hi):
                return hT[:, k, lo:hi]

            def prhs(k, lo, hi):
                return preT[:, k, lo:hi]

            def gated_out(wmat, gate_col0, mode):
                for g2 in range(2):
                    gslot, gi = pinned_k8(w_in[l], gate_col0 + g2 * 512, 512)
                    wslot, wi = pinned_k8(wmat, g2 * 512, 512)
                    for jj in range(4):
                        j = g2 * 4 + jj
                        Pg = fm_proj([gslot[:, k, jj * 128:(jj + 1) * 128] for k in range(8)], hrhs, geoE)
                        gate = v3(tmp[:, 0, 0:E], nse, sle)
                        act_op(gate, Pg, AF.Sigmoid, bias=V(l, V_BIN + gate_col0 // 128 + j))
                        Py = fm_proj([wslot[:, k, jj * 128:(jj + 1) * 128] for k in range(8)], prhs, geoP)
                        mj = v3(mrg[:, j, 0:E], nse, sle)
                        if mode == "set":
                            tt(mj, Py, gate, ALU.mult)
                        else:
                            t1 = v3(tmp[:, 1, 0:E], nse, sle)
                            tt(t1, Py, gate, ALU.mult)
                            tt(mj, mj, t1, ALU.add)
                        tick()
                    unpin(gi, wi)

            for c in range(8):
                slot = wload([
                    (lambda s, r=r: s[:, 0:3072].rearrange("p (k r n) -> p k r n", k=8, r=3)[:, :, r, :],
                     win_v[:, :, c + 8 * r, :]) for r in range(3)
                ])
                sv = slot[:, 0:3072].rearrange("p (k r n) -> p k r n", k=8, r=3)
                Pzx = fm_proj([sv[:, k, 2, :] for k in range(8)], hrhs, geoE)
                Pzc = fm_proj([sv[:, k, 1, :] for k in range(8)], hrhs, geoE)
                zx = v3(tmp[:, 0, 0:E], nse, sle)
                act_op(zx, Pzx, AF.Identity, bias=V(l, V_BIN + 16 + c))
                prodf = tmp[:, 1, 0:E]
                stt(v3(prodf, nse, sle), Pzc, V(l, V_BIN + 8 + c), zx, ALU.add, ALU.mult)
                if nmask:
                    ts(prodf[:, 0:nmask], prodf[:, 0:nmask], mask[:, 0:1], ALU.mult)
                Pzb = fm_proj([sv[:, k, 0, :] for k in range(8)], hrhs, geoE)
                cv = tmp[:, 2, 0:E]
                act_op(cv, prodf, AF.Identity, scale=V(l, V_CONVA + 16 + c))
                stt(cv[:, 2:E], prodf[:, 1:E - 1], V(l, V_CONVA + 8 + c), cv[:, 2:E], ALU.mult, ALU.add)
                stt(cv[:, 2:E], prodf[:, 0:E - 2], V(l, V_CONVA + 0 + c), cv[:, 2:E], ALU.mult, ALU.add)
                stt(v3(preT[:, c, 0:E], nse, sle), Pzb, V(l, V_BIN + c), v3(cv, nse, sle), ALU.add, ALU.mult)
                tick()
            gated_out(w_a_out[l], 6144, "set")

            zvslots = [wgrp_k8(w_in[l], 4096 + hf * 512, 512).rearrange("p (k n) -> p k n", k=8) for hf in range(2)]
            def v_s1(b):
                gv = tmp[:, 1 + b % 2, 0:1024]
                st = stat[:, (b % 2) * 16:(b % 2) * 16 + 16]
                for hf in range(2):
                    Pt = small_ps()
                    for k in range(8):
                        mm(Pt[:], hT[:, k, tcol0 + b * 128:tcol0 + (b + 1) * 128], zvslots[hf][:, k, :], k == 0, k == 7, k == 7)
                    gsl = gv[:, hf * 512:(hf + 1) * 512]
                    tt(gsl, Pt[:], bc[:, 0, hf * 512:(hf + 1) * 512], ALU.add)
                    act_op(gsl, gsl, AF.Gelu_apprx_tanh)
                for hf in range(2):
                    gsl = gv[:, hf * 512:(hf + 1) * 512]
                    R.op("dve", lambda e, hf=hf, gsl=gsl, st=st: e.bn_stats(out=st[:, hf * 6:(hf + 1) * 6], in_=gsl),
                         reads=(gsl,), writes=(st[:, hf * 6:(hf + 1) * 6],))

            def v_s2(b):
                gv = tmp[:, 1 + b % 2, 0:1024]
                st = stat[:, (b % 2) * 16:(b % 2) * 16 + 16]
                R.op("dve", lambda e: e.bn_aggr(out=st[:, 12:14], in_=st[:, 0:12].rearrange("p (a b) -> p a b", a=2)),
                     reads=(st[:, 0:12],), writes=(st[:, 12:14],))
                act_op(st[:, 14:15], st[:, 13:14], AF.Sqrt, bias=epsc[:, 0:1])
                R.op("dve", lambda e: e.reciprocal(out=st[:, 14:15], in_=st[:, 14:15]),
                     reads=(st[:, 14:15],), writes=(st[:, 14:15],))
                ts(tok[:, b, :], gv, st[:, 12:13], ALU.subtract, st[:, 14:15], ALU.mult)

            v_s1(0)
            for b in range(1, nb):
                v_s1(b)
                v_s2(b - 1)
            v_s2(nb - 1)
            for g2 in range(2):
                uslot, ui = pinned_k8(w_in[l], 3072 + g2 * 512, 512)
                for jj in range(4):
                    g = g2 * 4 + jj
                    Pu = fm_proj([uslot[:, k, jj * 128:(jj + 1) * 128] for k in range(8)], hrhs, geoE)
                    u = ubuf[:, g % 2, 0:E]
                    act_op(v3(u, nse, sle), Pu, AF.Gelu_apprx_tanh, bias=V(l, V_BIN + 24 + g))
                    Pm = big_ps()
                    Pmf = Pm[:].rearrange("p a b -> p (a b)")
                    for b in range(nb):
                        mm(Pmf[:, b * 128:(b + 1) * 128], tok[:, b, g * 128:(g + 1) * 128], wsT[:, l, g, :],
                           True, True, b == nb - 1)
                    mx = tmp[:, 0, 0:nb * 128]
                    stt(mx.rearrange("p (b i) -> p b i", b=nb), Pmf[:, 0:nb * 128].rearrange("p (b i) -> p b i", b=nb),
                        V(l, V_LNVG + g), bsb[:, g:g + 1, :].broadcast_to([128, nb, 128]), ALU.mult, ALU.add)
                    mset(preT[:, g, 0:lead], 0.0)
                    tt(preT[:, g, lead:E], mx[:, off:off + T], u[:, lead:E], ALU.mult)
                    tick()
                unpin(ui)
            gated_out(w_b_out[l], 7168, "add")

            zpslots = [wgrp_k8(w_in[l], 5120 + hf * 512, 512).rearrange("p (k n) -> p k n", k=8) for hf in range(2)]
            for b in range(nb):
                for hf in range(2):
                    Pt = small_ps()
                    for k in range(8):
                        mm(Pt[:], hT[:, k, tcol0 + b * 128:tcol0 + (b + 1) * 128], zpslots[hf][:, k, :], k == 0, k == 7, k == 7)
                    tt(tok[:, b, hf * 512:(hf + 1) * 512], Pt[:], bc[:, 1, hf * 512:(hf + 1) * 512], ALU.add)
            for c in range(8):
                kw = c // 2
                Pm = big_ps()
                Pmf = Pm[:].rearrange("p a b -> p (a b)")
                for b in range(nb):
                    band = bandf if (gtile0 + tb0 + b) == 2 else bandg
                    prev = pcar[:, l, c * 128:(c + 1) * 128] if b == 0 else tok[:, b - 1, c * 128:(c + 1) * 128]
                    o = Pmf[:, b * 128:(b + 1) * 128]
                    mm(o, tok[:, b, c * 128:(c + 1) * 128], band[:, kw, 0, :], True, False, False)
                    mm(o, prev, band[:, kw, 1, :], False, True, b == nb - 1)
                act_op(preT[:, c, lead:E], Pmf[:, off:off + T], AF.Identity)
            cp(pcar[:, l, :], tok[:, nb - 1, :])
            pslot_w = wload([
                (lambda s: s[:, 0:2048].rearrange("p (wk n) -> p wk n", wk=8),
                 w_pool[l].rearrange("w (kc p) n -> p (w kc) n", p=128)),
            ])[:, 0:2048].rearrange("p (w k n) -> p w k n", w=4, k=2)
            for g2 in range(2):
                gslot = wgrp_k8(w_in[l], 8192 + g2 * 512, 512).rearrange("p (k n) -> p k n", k=8)
                for jj in range(4):
                    j = g2 * 4 + jj
                    kw = j // 2
                    Pg = fm_proj([gslot[:, k, jj * 128:(jj + 1) * 128] for k in range(8)], hrhs, geoE)
                    gate = v3(tmp[:, 0, 0:E], nse, sle)
                    act_op(gate, Pg, AF.Sigmoid, bias=V(l, V_BIN + 64 + j))
                    Py = fm_proj([pslot_w[:, kw, kc, (j % 2) * 128:(j % 2 + 1) * 128] for kc in range(2)],
                                 lambda k, lo, hi, kw=kw: preT[:, 2 * kw + k, lo:hi], geoP)
                    t1 = v3(tmp[:, 1, 0:E], nse, sle)
                    stt(t1, Py, V(l, V_PSC + j), gate, ALU.mult, ALU.mult)
                    tt(mbf[:, j, 0:T], mrg[:, j, lead:E], tmp[:, 1, lead:E], ALU.add)

            drain()
            for g2 in range(2):
                wslot = wgrp_k8(w_o[l], g2 * 512, 512).rearrange("p (k n) -> p k n", k=8)
                for jj in range(4):
                    j = g2 * 4 + jj
                    Po = fm_proj([wslot[:, k, jj * 128:(jj + 1) * 128] for k in range(8)],
                                 lambda k, lo, hi: mbf[:, k, lo:hi], (0, ns, sl))
                    xv = v3(xT[:, j, c0:c0 + T], ns, sl)
                    stt(xv, Po, ada[:, l, 16 + j:17 + j], xv, ALU.mult, ALU.add)
                    ln_acc(j, c0, T)
            layer_norm_fm(c0, T, [
                (lambda c: xT[:, c, c0:c0 + T], lambda c: der[:, l, 4, c:c + 1], lambda c: der[:, l, 5, c:c + 1]),
                (lambda c: hT[:, c, hcol0:hcol0 + T], lambda c: der[:, l, 2, c:c + 1], lambda c: der[:, l, 3, c:c + 1]),
            ])

        def ffn(l, c0, T, gtile0, first_tile):
            lead, E, nse, sle, ns, sl, hcol0 = geom(c0, T)
            geoE = (hcol0 - lead, nse, sle)
            cp(hT[:, :, 0:LEAD], hcar[:, l, 1, :, :])
            cp(hcar[:, l, 1, :, :], hT[:, :, LEAD + TMAX - LEAD:LEAD + TMAX])
            nmask = (lead + max(0, 256 - (gtile0 * 128 + c0))) if first_tile else 0
            wup_v = w_up[l].rearrange("(kc p) (r n) -> p kc r n", p=128, r=2)

            def hrhs(k, lo, hi):
                return hT[:, k, lo:hi]

            for i in range(NFC // 2):
                slot = wload([
                    (lambda s, r=r: s[:, 0:4096].rearrange("p (k r n) -> p k r n", k=8, r=2)[:, :, r, :],
                     wup_v[:, :, r, i * 256:(i + 1) * 256]) for r in range(2)
                ])
                sv = slot[:, 0:4096].rearrange("p (k r n) -> p k r n", k=8, r=2)
                for jj in range(2):
                    c = 2 * i + jj
                    Pa = fm_proj([sv[:, k, 0, jj * 128:(jj + 1) * 128] for k in range(8)], hrhs, geoE)
                    af = tmp[:, 0, 0:E]
                    act_op(v3(af, nse, sle), Pa, AF.Identity, bias=V(l, V_BUP + c))
                    if nmask:
                        ts(af[:, 0:nmask], af[:, 0:nmask], mask[:, 0:1], ALU.mult)
                    cv = tmp[:, 1, 0:E]
                    act_op(cv, af, AF.Identity, scale=V(l, V_CONVF + 44 + c))
                    stt(cv[:, 2:E], af[:, 1:E - 1], V(l, V_CONVF + 22 + c), cv[:, 2:E], ALU.mult, ALU.add)
                    stt(cv[:, 2:E], af[:, 0:E - 2], V(l, V_CONVF + 0 + c), cv[:, 2:E], ALU.mult, ALU.add)
                    gl = tmp[:, 2, 0:E]
                    act_op(gl[:, 2:E], cv[:, 2:E], AF.Gelu_apprx_tanh, bias=V(l, V_CONVFB + c))
                    Pg = fm_proj([sv[:, k, 1, jj * 128:(jj + 1) * 128] for k in range(8)], hrhs, geoE)
                    stt(v3(fT[:, c, 0:E], nse, sle), Pg, V(l, V_BUP + NFC + c), v3(gl, nse, sle), ALU.add, ALU.mult)
            wd_v = w_down[l].rearrange("(kc p) n -> p kc n", p=128)
            for jp in range(4):
                Ps = [big_ps(), big_ps()]
                for kh in range(2):
                    slot = wload([
                        (lambda s: s[:, 0:2816].rearrange("p (k n) -> p k n", k=11),
                         wd_v[:, kh * 11:(kh + 1) * 11, jp * 256:(jp + 1) * 256]),
                    ])
                    sv = slot[:, 0:2816].rearrange("p (k n) -> p k n", k=11)
                    for jj in range(2):
                        for s in range(ns):
                            o = Ps[jj][:, s, 0:sl]
                            for k in range(11):
                                mm(o, sv[:, k, jj * 128:(jj + 1) * 128],
                                   fT[:, kh * 11 + k, lead + s * sl:lead + (s + 1) * sl],
                                   kh == 0 and k == 0, kh == 1 and k == 10, s == ns - 1 and k == 10)
                for jj in range(2):
                    j = jp * 2 + jj
                    xv = v3(xT[:, j, c0:c0 + T], ns, sl)
                    stt(xv, Ps[jj][:, 0:ns, 0:sl], ada[:, l, 40 + j:41 + j], xv, ALU.mult, ALU.add)
                    ln_acc(j, c0, T)
            if l < DEPTH - 1:
                layer_norm_fm(c0, T, [
                    (lambda c: xT[:, c, c0:c0 + T], lambda c: der[:, l, 6, c:c + 1], lambda c: der[:, l, 7, c:c + 1]),
                    (lambda c: hT[:, c, hcol0:hcol0 + T], lambda c: der[:, l + 1, 0, c:c + 1], lambda c: der[:, l + 1, 1, c:c + 1]),
                ])
            else:
                layer_norm_fm(c0, T, [
                    (lambda c: xT[:, c, c0:c0 + T], lambda c: V(l, V_LN2G + c), lambda c: V(l, V_LN2B + c)),
                ])

        iost = {"i": 0, "o": 0}

        def load_x(gblk_list):
            for bi, gb in enumerate(gblk_list):
                s = iost["i"] % 2
                iost["i"] += 1
                dma("sp", xin[:, s, :], xs[gb * 128:(gb + 1) * 128, :], "xin%d" % s, writes=(xin[:, s, :],))
                for half, P in enumerate((PC, PD)):
                    for cc in range(4):
                        c = half * 4 + cc
                        o = P[:, cc * 128:(cc + 1) * 128]
                        i_ap = xin[:, s, c * 128:(c + 1) * 128]
                        R.op("pe", lambda e, o=o, i_ap=i_ap: e.transpose(o, i_ap, ident[:]),
                             reads=(i_ap, ident[:]), writes=(o,), signal=(cc == 3))
                    ts(xT[:, half * 4:half * 4 + 4, bi * 128:(bi + 1) * 128],
                       P[:, 0:512].rearrange("p (c t) -> p c t", c=4), ALPHA, ALU.mult)

        def make_h0(T):
            for c in range(8):
                act_op(hT[:, c, LEAD:LEAD + T], xT[:, c, 0:T], AF.Identity,
                       bias=der[:, 0, 1, c:c + 1], scale=der[:, 0, 0, c:c + 1])

        def store_y(col_blocks):
            for b, ob in col_blocks:
                s = iost["o"] % 2
                iost["o"] += 1
                for half, P in enumerate((PC, PD)):
                    for cc in range(4):
                        c = half * 4 + cc
                        o = P[:, cc * 128:(cc + 1) * 128]
                        i_ap = xT[:, c, b * 128:(b + 1) * 128]
                        R.op("pe", lambda e, o=o, i_ap=i_ap: e.transpose(o, i_ap, ident[:]),
                             reads=(i_ap, ident[:]), writes=(o,), signal=(cc == 3))
                    if half == 0:
                        act_op(ost[:, s, 0:512], P[:, 0:512], AF.Identity)
                    else:
                        cp(ost[:, s, 512:1024], P[:, 0:512])
                dma("sp", y[ob * 128:(ob + 1) * 128, :], ost[:, s, :], "ost%d" % s, reads=(ost[:, s, :],))

        load_x(list(range(0, 9)))
        for g in range(4):
            ada_step(0, g)
        der_part1(0)
        make_h0(1152)
        for g in range(4, 12):
            todo.append(lambda g=g: ada_step(0, g))
        todo.append(lambda: der_part2(0))
        for g in range(12):
            todo.append(lambda g=g: ada_step(1, g))
        todo.append(lambda: der_part1(1))
        todo.append(lambda: der_part2(1))
        mixing(0, 120, 1032, 0, True)
        ffn(0, 120, 1032, 0, True)
        mixing(1, 248, 904, 0, True)
        ffn(1, 248, 904, 0, True)
        store_y([(b, b - 2) for b in range(2, 9)])
        load_x(list(range(9, 18)))
        make_h0(1152)
        mixing(0, 0, 1152, 9, False)
        ffn(0, 0, 1152, 9, False)
        mixing(1, 0, 1152, 9, False)
        ffn(1, 0, 1152, 9, False)
        store_y([(b, 7 + b) for b in range(0, 9)])
        R.wait_all("sp", [("ost0", R.val["ost0"]), ("ost1", R.val["ost1"])])

        with nc.Block() as block:
            @block.tensor
            def _(e):
                R.replay(e, "pe")

            @block.scalar
            def _(e):
                R.replay(e, "act")

            @block.vector
            def _(e):
                R.replay(e, "dve")

            @block.gpsimd
            def _(e):
                R.replay(e, "pool")

            @block.sync
            def _(e):
                R.replay(e, "sp")
    return nc


def _fm(v, nch):
    return np.ascontiguousarray(np.asarray(v, np.float32).reshape(nch, 128).T)


def _bands(first):
    out = np.zeros((128, 4, 2, 128), np.float32)
    for k, w in enumerate(POOL_W):
        for t in range(128):
            if first:
                n = min(t + 1, w)
                for tp in range(max(0, t - w + 1), t + 1):
                    out[tp, k, 0, t] += 1.0 / n
            else:
                for tp in range(t - w + 1, t + 1):
                    if tp >= 0:
                        out[tp, k, 0, t] += 1.0 / w
                    else:
                        out[tp + 128, k, 1, t] += 1.0 / w
            out[t, k, 0, t] -= 1.0
    return out.reshape(128, 4 * 2 * 128)


_CACHE = {}


def kernel(x, c, w_ada, b_ada, w_in, b_in, conv_a, w_a_out, ln_v_g, ln_v_b,
           w_spatial, b_spatial, w_b_out, w_pool, pool_scale, w_o, ln1_g, ln1_b,
           w_up, b_up, conv_ffn, conv_ffn_b, w_down, ln2_g, ln2_b):
    f = lambda a: np.ascontiguousarray(np.asarray(a, dtype=np.float32))
    x = f(x)
    c = f(c)
    vecs = np.zeros((128, DEPTH, NV), np.float32)
    rows = np.zeros((DEPTH, 128, 2, D), np.float32)
    bsb = np.zeros((DEPTH, 128, 8, 128), np.float32)
    wsT = np.zeros((128, DEPTH, 8, 128), np.float32)
    for l in range(DEPTH):
        vecs[:, l, V_BIN:V_BIN + 72] = _fm(b_in[l], 72)
        vecs[:, l, V_BADA:V_BADA + 48] = _fm(b_ada[l], 48)
        vecs[:, l, V_CONVA:V_CONVA + 24] = np.concatenate([_fm(conv_a[l][k], 8) for k in range(3)], axis=1)
        vecs[:, l, V_LN1G:V_LN1G + 8] = _fm(ln1_g[l], 8)
        vecs[:, l, V_LN1B:V_LN1B + 8] = _fm(ln1_b[l], 8)
        vecs[:, l, V_LN2G:V_LN2G + 8] = _fm(ln2_g[l], 8)
        vecs[:, l, V_LN2B:V_LN2B + 8] = _fm(ln2_b[l], 8)
        vecs[:, l, V_PSC:V_PSC + 8] = _fm(pool_scale[l], 8)
        vecs[:, l, V_BUP:V_BUP + 44] = _fm(b_up[l], 44)
        vecs[:, l, V_CONVF:V_CONVF + 66] = np.concatenate([_fm(conv_ffn[l][k], NFC) for k in range(3)], axis=1)
        vecs[:, l, V_CONVFB:V_CONVFB + NFC] = _fm(conv_ffn_b[l], NFC)
        vecs[:, l, V_LNVG:V_LNVG + 8] = _fm(ln_v_g[l], 8)
        vecs[:, l, V_LNVB:V_LNVB + 8] = _fm(ln_v_b[l], 8)
        bl = np.asarray(b_in[l], np.float32)
        rows[l, :, 0, :] = bl[4096:5120][None, :]
        rows[l, :, 1, :] = bl[5120:6144][None, :]
        bsb[l] = np.asarray(b_spatial[l], np.float32)[None, :, :]
        wsT[:, l] = np.transpose(np.asarray(w_spatial[l], np.float32), (2, 0, 1))
    wsT = np.ascontiguousarray(wsT.reshape(128, DEPTH * 8 * 128))
    bandg = _bands(False)
    bandf = _bands(True)
    ident = np.eye(128, dtype=np.float32)
    shared = {
        "vecs": vecs, "rows": rows, "bsb": bsb, "wsT": wsT, "bandg": bandg, "ident": ident,
        "w_ada": f(w_ada), "w_in": f(w_in), "w_a_out": f(w_a_out), "w_b_out": f(w_b_out),
        "w_pool": f(w_pool), "w_o": f(w_o), "w_up": f(w_up), "w_down": f(w_down),
    }
    in_maps = []
    for core in range(8):
        b, half = core // 2, core % 2
        xs = np.zeros((TOK_IN, D), np.float32)
        if half == 0:
            xs[256:] = x[b, 0:2048]
        else:
            xs[:] = x[b, 2048 - 256:4096]
        m = dict(shared)
        m["xs"] = xs
        m["cvec"] = _fm(c[b], 8)
        m["mask"] = np.full((128, 1), float(half), np.float32)
        m["bandf"] = bandf if half == 0 else bandg
        in_maps.append(m)
    if "nc" not in _CACHE:
        _CACHE["nc"] = build_program()
    res = run_bass_kernel_spmd(_CACHE["nc"], in_maps, core_ids=list(range(8)))
    out = np.zeros((NBATCH, SEQ, D), np.float32)
    for core in range(8):
        b, half = core // 2, core % 2
        out[b, half * 2048:(half + 1) * 2048] = res.results[core]["y"]
    return out
```

```python
import numpy as np
from contextlib import ExitStack
import concourse.bass as bass
import concourse.mybir as mybir
from concourse.bass_utils import run_bass_kernel_spmd

F32 = mybir.dt.float32
BF16 = mybir.dt.bfloat16
AF = mybir.ActivationFunctionType
ALU = mybir.AluOpType
ESZ = {F32: 4, BF16: 2}
PSUM_NAMES = ("PA", "PB", "PC", "PD")

D = 1024
SEQ = 4096
NBATCH = 4
DEPTH = 2
DFF = 2816
NFC = DFF // 128
DIN = 9216
ALPHA = (2 * DEPTH) ** 0.25
EPS = 1e-5
POOL_W = (2, 4, 8, 16)
NBLK_IN = 18
TOK_IN = NBLK_IN * 128
TOK_OUT = 2048
TMAX = 1152
LEAD = 6
EMAX = TMAX + LEAD
NSLOT = 4
SLOT_ELEMS = 4096

V_BIN = 0
V_BADA = 72
V_CONVA = 120
V_LN1G = 144
V_LN1B = 152
V_LN2G = 160
V_LN2B = 168
V_PSC = 176
V_BUP = 184
V_CONVF = 228
V_CONVFB = 294
V_LNVG = 316
V_LNVB = 324
NV = 332


def _rng(ap):
    a = ap.ap
    esz = ESZ[ap.dtype]
    pstep = a[0][0]
    off = ap.offset % pstep if pstep > 0 else ap.offset
    span = 1
    for step, cnt in a[1:]:
        span += (cnt - 1) * abs(step)
    lo, hi = off * esz, (off + span) * esz
    if ap.tensor.name in PSUM_NAMES:
        lo = (lo // 2048) * 2048
        hi = -(-hi // 2048) * 2048
    return ap.tensor.name, lo, hi


class Rec:
    def __init__(self, nc, es):
        self.nc = nc
        self.es = es
        self.engs = ["pe", "act", "dve", "pool", "sp"]
        self.sems = {}
        self.val = {}
        self.waited = {e: {} for e in self.engs}
        self.ops = {e: [] for e in self.engs}
        self.tr = {}
        self.pending = {e: ([], []) for e in self.engs}
        for e in self.engs:
            self.newsem(e)

    def newsem(self, name):
        self.sems[name] = self.es.enter_context(self.nc.semaphore("s_" + name))
        self.val[name] = 0

    def _need(self, reads, writes):
        need = {}

        def add(tok):
            if tok is None:
                return
            s, v = tok
            if need.get(s, 0) < v:
                need[s] = v

        for ap in reads:
            name, lo, hi = _rng(ap)
            for (l2, h2), rec in self.tr.get(name, {}).items():
                if l2 < hi and lo < h2:
                    add(rec[0])
        for ap in writes:
            name, lo, hi = _rng(ap)
            for (l2, h2), rec in self.tr.get(name, {}).items():
                if l2 < hi and lo < h2:
                    add(rec[0])
                    for s, v in rec[1].items():
                        add((s, v))
        return need

    def _record(self, reads, writes, tok):
        for ap in reads:
            name, lo, hi = _rng(ap)
            d = self.tr.setdefault(name, {})
            rec = d.setdefault((lo, hi), [None, {}])
            if rec[1].get(tok[0], 0) < tok[1]:
                rec[1][tok[0]] = tok[1]
        for ap in writes:
            name, lo, hi = _rng(ap)
            d = self.tr.setdefault(name, {})
            for k in [k for k in d if lo <= k[0] and k[1] <= hi]:
                del d[k]
            d[(lo, hi)] = [tok, {}]

    def op(self, eng, fn, reads=(), writes=(), sem=None, inc=1, signal=True, rec_val=None):
        need = self._need(reads, writes)
        waits = []
        for s, v in need.items():
            if s == "pe" and eng == "pe":
                continue
            if self.waited[eng].get(s, 0) >= v:
                continue
            self.waited[eng][s] = v
            waits.append((s, v))
        tok = None
        if sem is not None:
            self.val[sem] += inc
            tok = (sem, self.val[sem])
            self._record(reads, writes, (sem, rec_val) if rec_val is not None else tok)
        else:
            pr, pw = self.pending[eng]
            pr.extend(reads)
            pw.extend(writes)
            if signal:
                self.val[eng] += inc
                tok = (eng, self.val[eng])
                self._record(pr, pw, tok)
                self.pending[eng] = ([], [])
        self.ops[eng].append((waits, fn, tok, inc))

    def dma_group(self, eng, sem, items):
        allr = [a for _, r, _ in items for a in r]
        allw = [a for _, _, w in items for a in w]
        need = self._need(allr, allw)
        waits = []
        for s, v in need.items():
            if self.waited[eng].get(s, 0) >= v:
                continue
            self.waited[eng][s] = v
            waits.append((s, v))
        final = self.val[sem] + 16 * len(items)
        for i, (fn, r, w) in enumerate(items):
            self.val[sem] += 16
            self.ops[eng].append((waits if i == 0 else [], fn, (sem, self.val[sem]), 16))
        self._record(allr, allw, (sem, final))

    def wait_all(self, eng, toks):
        waits = [(s, v) for s, v in toks if self.waited[eng].get(s, 0) < v]
        for s, v in waits:
            self.waited[eng][s] = v
        self.ops[eng].append((waits, None, None, 0))

    def replay(self, e, eng):
        for waits, fn, tok, inc in self.ops[eng]:
            for s, v in waits:
                e.wait_ge(self.sems[s], v)
            if fn is None:
                continue
            ins = fn(e)
            if tok is not None:
                ins.then_inc(self.sems[tok[0]], inc)


def _split(n, maxn=512):
    k = -(-n // maxn)
    while n % k or (n // k) % 2:
        k += 1
    return k, n // k


def build_program():
    nc = bass.Bass("TRN2", target_bir_lowering=False)
    es = ExitStack()
    with es:
        def din(name, shape):
            return nc.dram_tensor(name, list(shape), F32, kind="ExternalInput").ap()

        xs = din("xs", [TOK_IN, D])
        cvec_d = din("cvec", [128, 8])
        mask_d = din("mask", [128, 1])
        vecs_d = din("vecs", [128, DEPTH, NV])
        rows_d = din("rows", [DEPTH, 128, 2, D])
        bsb_d = din("bsb", [DEPTH, 128, 8, 128])
        wsT_d = din("wsT", [128, DEPTH * 8 * 128])
        bandg_d = din("bandg", [128, 4 * 2 * 128])
        bandf_d = din("bandf", [128, 4 * 2 * 128])
        ident_d = din("ident", [128, 128])
        w_ada = din("w_ada", [DEPTH, D, 6 * D])
        w_in = din("w_in", [DEPTH, D, DIN])
        w_a_out = din("w_a_out", [DEPTH, D, D])
        w_b_out = din("w_b_out", [DEPTH, D, D])
        w_pool = din("w_pool", [DEPTH, 4, 256, 256])
        w_o = din("w_o", [DEPTH, D, D])
        w_up = din("w_up", [DEPTH, D, 2 * DFF])
        w_down = din("w_down", [DEPTH, DFF, D])
        y = nc.dram_tensor("y", [TOK_OUT, D], F32, kind="ExternalOutput").ap()

        def sb(name, shape, dt):
            return es.enter_context(nc.sbuf_tensor(name, list(shape), dt))

        def ps(name, shape):
            return es.enter_context(nc.psum_tensor(name, list(shape), F32))

        xT = sb("xT", [128, 8, TMAX], F32)
        hT = sb("hT", [128, 8, EMAX], BF16)
        preT = sb("preT", [128, 8, EMAX], BF16)
        ubuf = sb("ubuf", [128, 2, EMAX], BF16)
        BIG_MRG = 8 * EMAX * 4
        BIG_BYTES = BIG_MRG + 9 * 1024 * 2
        big = sb("big", [128, BIG_BYTES // 2], BF16)
        tmp = sb("tmp", [128, 3, EMAX], F32)
        gv = tmp[:, 2, 0:1024]
        wsl = sb("wsl", [128, NSLOT, SLOT_ELEMS], BF16)
        bc = sb("bc", [128, 2, D], F32)
        bsb = sb("bsb_s", [128, 8, 128], F32)
        vecs = sb("vecs_s", [128, DEPTH, NV], F32)
        ada = sb("ada", [128, DEPTH, 48], F32)
        der = sb("der", [128, DEPTH, 8, 8], F32)
        cvec = sb("cvec_s", [128, 8], F32)
        cact = sb("cact", [128, 8], BF16)
        mask = sb("mask_s", [128, 1], F32)
        wsT = sb("wsT_s", [128, DEPTH, 8, 128], BF16)
        bandg = sb("bandg_s", [128, 4, 2, 128], BF16)
        bandf = sb("bandf_s", [128, 4, 2, 128], BF16)
        ident = sb("ident_s", [128, 128], F32)
        ones = sb("ones_s", [128, 128], BF16)
        pcar = sb("pcar", [128, DEPTH, D], BF16)
        hcar = sb("hcar", [128, DEPTH, 2, 8, LEAD], BF16)
        stat = sb("stat", [128, 32], F32)
        sc8 = sb("sc8", [128, 16], F32)
        epsc = sb("epsc", [128, 1], F32)
        xin = tmp[:, 0:2, 0:D]
        ost = tmp[:, 0:2, 0:D]

        PA = ps("PA", [128, 3, 512])
        PB = ps("PB", [128, 3, 512])
        PC = ps("PC", [128, 512])
        PD = ps("PD", [128, 512])

        mrg = big[:, 0:BIG_MRG // 2].bitcast(F32).rearrange("p (c t) -> p c t", c=8)
        tok = big[:, BIG_MRG // 2:BIG_BYTES // 2].rearrange("p (b f) -> p b f", b=9)
        fT = big[:, 0:NFC * EMAX].rearrange("p (c t) -> p c t", c=NFC)
        mbf = big[:, BIG_MRG // 2:BIG_BYTES // 2].rearrange("p (c t) -> p c t", c=8)

        R = Rec(nc, es)
        for i in range(NSLOT):
            R.newsem("w%d" % i)
        for n in ["cst", "cstp", "xin0", "xin1", "ost0", "ost1", "bc", "bsb"]:
            R.newsem(n)

        wstate = {"n": 0, "pinned": set(), "last": None}

        def wload(pieces, pin=False):
            while True:
                i = wstate["n"] % NSLOT
                wstate["n"] += 1
                if i not in wstate["pinned"]:
                    break
            if pin:
                wstate["pinned"].add(i)
            wstate["last"] = i
            slot = wsl[:, i, :]
            items = []
            for dstf, src in pieces:
                dst = dstf(slot)
                items.append((lambda e, dst=dst, src=src: e.dma_start(out=dst, in_=src), (), (dst,)))
            R.dma_group("pool", "w%d" % i, items)
            return slot

        def wgrp_k8(src2d, c0, ncols, pin=False):
            src = src2d.rearrange("(kc p) n -> p kc n", p=128)[:, :, c0:c0 + ncols]
            return wload([(lambda s: s[:, 0:8 * ncols].rearrange("p (k n) -> p k n", k=8), src)], pin=pin)

        def pinned_k8(src2d, c0, ncols):
            sl = wgrp_k8(src2d, c0, ncols, pin=True)
            return sl.rearrange("p (k n) -> p k n", k=8), wstate["last"]

        def unpin(*idx):
            for i in idx:
                wstate["pinned"].discard(i)

        pslot = {"n": 0, "m": 0}

        def big_ps():
            pslot["n"] += 1
            return PA if pslot["n"] % 2 else PB

        def small_ps():
            pslot["m"] += 1
            return PC if pslot["m"] % 2 else PD

        def mm(o, l, r, start, stop, signal):
            R.op("pe", lambda e: e.matmul(o, l, r, start=start, stop=stop), reads=(l, r), writes=(o,), signal=signal)

        def fm_proj(lhs_list, rhs_fn, geo, P=None):
            off, nseg, seglen = geo
            if P is None:
                P = big_ps()
            nk = len(lhs_list)
            for j in range(nseg):
                lo = off + j * seglen
                o = P[:, j, 0:seglen]
                for k in range(nk):
                    mm(o, lhs_list[k], rhs_fn(k, lo, lo + seglen), k == 0, k == nk - 1,
                       (j == nseg - 1) and (k == nk - 1))
            return P[:, 0:nseg, 0:seglen]

        def V(l, col, n=1):
            return vecs[:, l, col:col + n]

        def act_op(out, in_, func, bias=0.0, scale=1.0):
            rd = [in_] + [a for a in (bias, scale) if not isinstance(a, float)]
            R.op("act", lambda e: e.activation(out=out, in_=in_, func=func, bias=bias, scale=scale),
                 reads=rd, writes=(out,))

        def tt(out, in0, in1, op, eng="dve"):
            R.op(eng, lambda e: e.tensor_tensor(out=out, in0=in0, in1=in1, op=op), reads=(in0, in1), writes=(out,))

        def ts(out, in0, s1, op0, s2=None, op1=None, eng="dve"):
            rd = [in0] + [a for a in (s1, s2) if a is not None and not isinstance(a, float)]
            if op1 is None:
                R.op(eng, lambda e: e.tensor_scalar(out=out, in0=in0, scalar1=s1, scalar2=None, op0=op0),
                     reads=rd, writes=(out,))
            else:
                R.op(eng, lambda e: e.tensor_scalar(out=out, in0=in0, scalar1=s1, scalar2=s2, op0=op0, op1=op1),
                     reads=rd, writes=(out,))

        def stt(out, in0, scalar, in1, op0, op1, eng="dve"):
            rd = [in0, in1] + ([] if isinstance(scalar, float) else [scalar])
            R.op(eng, lambda e: e.scalar_tensor_tensor(out=out, in0=in0, scalar=scalar, in1=in1, op0=op0, op1=op1),
                 reads=rd, writes=(out,))

        def cp(out, in_, eng="dve"):
            R.op(eng, lambda e: e.tensor_copy(out=out, in_=in_), reads=(in_,), writes=(out,))

        def mset(ap, val, eng="dve"):
            R.op(eng, lambda e: e.memset(ap, val), writes=(ap,))

        def dma(eng, out, in_, sem, reads=(), writes=()):
            R.op(eng, lambda e: e.dma_start(out=out, in_=in_), reads=reads, writes=writes, sem=sem, inc=16)

        def v3(ap2d, nseg, seglen):
            return ap2d.rearrange("p (s n) -> p s n", s=nseg)

        dma("sp", vecs[:], vecs_d, "cst", writes=(vecs[:],))
        dma("sp", cvec[:], cvec_d, "cst", writes=(cvec[:],))
        dma("sp", mask[:], mask_d, "cst", writes=(mask[:],))
        dma("sp", ident[:], ident_d, "cst", writes=(ident[:],))
        for l in range(DEPTH):
            dma("pool", wsT[:, l].rearrange("p b c -> p (b c)"), wsT_d[:, l * 1024:(l + 1) * 1024], "cstp", writes=(wsT[:, l],))
        dma("pool", bandg[:].rearrange("p a b c -> p (a b c)"), bandg_d, "cstp", writes=(bandg[:],))
        dma("pool", bandf[:].rearrange("p a b c -> p (a b c)"), bandf_d, "cstp", writes=(bandf[:],))
        for name in list(R.tr.keys()):
            for k, rec in R.tr[name].items():
                if rec[0] is not None and rec[0][0] in ("cst", "cstp"):
                    rec[0] = (rec[0][0], R.val[rec[0][0]])

        mset(ones[:], 1.0)
        mset(epsc[:], EPS)
        mset(hT[:], 0.0)
        mset(pcar[:], 0.0)
        mset(hcar[:], 0.0)
        mset(tmp[:], 0.0)
        for l in range(DEPTH):
            R.op("dve", lambda e, l=l: e.memset(wsT[64:128, l, :, 0:64], 0.0), writes=(wsT[:, l, :, :],))
        act_op(cact[:], cvec[:], AF.Silu)

        def ada_step(l, g):
            Pc = small_ps()
            slot = wgrp_k8(w_ada[l], g * 512, 512)
            wv = slot.rearrange("p (k n) -> p k n", k=8)
            for jj in range(4):
                for k in range(8):
                    mm(Pc[:, jj:jj + 1], wv[:, k, jj * 128:(jj + 1) * 128], cact[:, k:k + 1], k == 0, k == 7, k == 7)
            tt(ada[:, l, 4 * g:4 * g + 4], Pc[:, 0:4], V(l, V_BADA + 4 * g, 4), ALU.add)

        def der_part1(l):
            sh1, sc1 = ada[:, l, 0:8], ada[:, l, 8:16]
            t1 = sc8[:, 0:8]
            ts(t1, sc1, 1.0, ALU.add)
            if l == 0:
                ts(der[:, l, 0, :], t1, 1.0 / ALPHA, ALU.mult)
                cp(der[:, l, 1, :], sh1)
            else:
                tt(der[:, l, 0, :], t1, V(l - 1, V_LN2G, 8), ALU.mult)
                tt(der[:, l, 1, :], t1, V(l - 1, V_LN2B, 8), ALU.mult)
                tt(der[:, l, 1, :], der[:, l, 1, :], sh1, ALU.add)

        def der_part2(l):
            sh2, sc2 = ada[:, l, 24:32], ada[:, l, 32:40]
            t2 = sc8[:, 8:16]
            ts(t2, sc2, 1.0, ALU.add)
            tt(der[:, l, 2, :], t2, V(l, V_LN1G, 8), ALU.mult)
            tt(der[:, l, 3, :], t2, V(l, V_LN1B, 8), ALU.mult)
            tt(der[:, l, 3, :], der[:, l, 3, :], sh2, ALU.add)
            ts(der[:, l, 4, :], V(l, V_LN1G, 8), ALPHA, ALU.mult)
            ts(der[:, l, 5, :], V(l, V_LN1B, 8), ALPHA, ALU.mult)
            ts(der[:, l, 6, :], V(l, V_LN2G, 8), ALPHA, ALU.mult)
            ts(der[:, l, 7, :], V(l, V_LN2B, 8), ALPHA, ALU.mult)

        todo = []

        def tick(n=1):
            for _ in range(n):
                if todo:
                    todo.pop(0)()

        def drain():
            while todo:
                todo.pop(0)()

        def ln_acc(j, c0, T):
            r = xT[:, j, c0:c0 + T]
            rsum, rsq = tmp[:, 1, 0:T], tmp[:, 2, 0:T]
            sqt = tmp[:, 0, 0:T]
            if j == 0:
                act_op(rsum, r, AF.Identity)
                act_op(rsq, r, AF.Square)
            elif j < 7:
                tt(rsum, rsum, r, ALU.add)
                act_op(sqt, r, AF.Square)
                tt(rsq, rsq, sqt, ALU.add)
            else:
                tt(ubuf[:, 0, 0:T], rsum, r, ALU.add)
                act_op(sqt, r, AF.Square)
                tt(ubuf[:, 1, 0:T], rsq, sqt, ALU.add)

        def layer_norm_fm(c0, T, outs):
            nseg, seglen = _split(T)
            rbc = ubuf[:, 0, 0:T]
            rsc = ubuf[:, 1, 0:T]
            S1 = big_ps()
            S2 = big_ps()
            for j in range(nseg):
                lo = j * seglen
                mm(S1[:, j, 0:seglen], ones[:], rbc[:, lo:lo + seglen], True, True, False)
                mm(S2[:, j, 0:seglen], ones[:], rsc[:, lo:lo + seglen], True, True, j == nseg - 1)
            mean = v3(tmp[:, 1, 0:T], nseg, seglen)
            rstd = v3(tmp[:, 2, 0:T], nseg, seglen)
            ts(mean, S1[:, 0:nseg, 0:seglen], 1.0 / D, ALU.mult)
            tt(rstd, mean, mean, ALU.mult)
            stt(rstd, S2[:, 0:nseg, 0:seglen], 1.0 / D, rstd, ALU.mult, ALU.subtract)
            act_op(rstd, rstd, AF.Sqrt, bias=epsc[:, 0:1])
            R.op("dve", lambda e: e.reciprocal(out=rstd, in_=rstd), reads=(rstd,), writes=(rstd,))
            meanf, rstdf = tmp[:, 1, 0:T], tmp[:, 2, 0:T]
            for c in range(8):
                r = xT[:, c, c0:c0 + T]
                yh = tmp[:, 0, 0:T] if c % 2 == 0 else ubuf[:].rearrange("p a b -> p (a b)").bitcast(F32)[:, 0:T]
                tt(yh, r, meanf, ALU.subtract)
                tt(yh, yh, rstdf, ALU.mult)
                for dst_fn, sc_fn, b_fn in outs:
                    act_op(dst_fn(c), yh, AF.Identity, bias=b_fn(c), scale=sc_fn(c))

        def geom(c0, T):
            for lead in (2, 4, 6):
                if _split(T + lead)[0] <= 3:
                    break
            else:
                raise AssertionError("no segment geometry for T=%d" % T)
            assert _split(T)[0] <= 3
            E = T + lead
            nse, sle = _split(E)
            ns, sl = _split(T)
            hcol0 = LEAD + c0
            return lead, E, nse, sle, ns, sl, hcol0

        def mixing(l, c0, T, gtile0, first_tile):
            tb0, off = c0 // 128, c0 % 128
            nb = 9 - tb0
            assert off + T == nb * 128
            tcol0 = LEAD + tb0 * 128
            lead, E, nse, sle, ns, sl, hcol0 = geom(c0, T)
            geoE = (hcol0 - lead, nse, sle)
            geoP = (0, nse, sle)
            geoT = (lead, ns, sl)
            cp(hT[:, :, 0:LEAD], hcar[:, l, 0, :, :])
            cp(hcar[:, l, 0, :, :], hT[:, :, LEAD + TMAX - LEAD:LEAD + TMAX])
            dma("sp", bc[:], rows_d[l], "bc", writes=(bc[:],))
            dma("sp", bsb[:], bsb_d[l], "bsb", writes=(bsb[:],))
            for hh in range(2):
                Pr = small_ps()
                mm(Pr[:], ones[:], wsT[:, l, 4 * hh:4 * hh + 4, :].rearrange("p a b -> p (a b)"), True, True, True)
                for gg in range(4):
                    g = 4 * hh + gg
                    stt(bsb[:, g, :], Pr[:, gg * 128:(gg + 1) * 128], V(l, V_LNVB + g), bsb[:, g, :], ALU.mult, ALU.add)
            nmask = (lead + max(0, 256 - (gtile0 * 128 + c0))) if first_tile else 0
            win_v = w_in[l].rearrange("(kc p) (r n) -> p kc r n", p=128, n=128)

            def hrhs(k, lo, hi):
                return hT[:, k, lo:hi]

            def prhs(k, lo, hi):
                return preT[:, k, lo:hi]

            def gated_out(wmat, gate_col0, mode):
                for g2 in range(2):
                    gslot, gi = pinned_k8(w_in[l], gate_col0 + g2 * 512, 512)
                    wslot, wi = pinned_k8(wmat, g2 * 512, 512)
                    for jj in range(4):
                        j = g2 * 4 + jj
                        Pg = fm_proj([gslot[:, k, jj * 128:(jj + 1) * 128] for k in range(8)], hrhs, geoE)
                        gate = v3(tmp[:, 0, 0:E], nse, sle)
                        act_op(gate, Pg, AF.Sigmoid, bias=V(l, V_BIN + gate_col0 // 128 + j))
                        Py = fm_proj([wslot[:, k, jj * 128:(jj + 1) * 128] for k in range(8)], prhs, geoP)
                        mj = v3(mrg[:, j, 0:E], nse, sle)
                        if mode == "set":
                            tt(mj, Py, gate, ALU.mult)
                        else:
                            t1 = v3(tmp[:, 1, 0:E], nse, sle)
                            tt(t1, Py, gate, ALU.mult)
                            tt(mj, mj, t1, ALU.add)
                        tick()
                    unpin(gi, wi)

            for c in range(8):
                slot = wload([
                    (lambda s, r=r: s[:, 0:3072].rearrange("p (k r n) -> p k r n", k=8, r=3)[:, :, r, :],
                     win_v[:, :, c + 8 * r, :]) for r in range(3)
                ])
                sv = slot[:, 0:3072].rearrange("p (k r n) -> p k r n", k=8, r=3)
                Pzx = fm_proj([sv[:, k, 2, :] for k in range(8)], hrhs, geoE)
                Pzc = fm_proj([sv[:, k, 1, :] for k in range(8)], hrhs, geoE)
                zx = v3(tmp[:, 0, 0:E], nse, sle)
                act_op(zx, Pzx, AF.Identity, bias=V(l, V_BIN + 16 + c))
                prodf = tmp[:, 1, 0:E]
                stt(v3(prodf, nse, sle), Pzc, V(l, V_BIN + 8 + c), zx, ALU.add, ALU.mult)
                if nmask:
                    ts(prodf[:, 0:nmask], prodf[:, 0:nmask], mask[:, 0:1], ALU.mult)
                Pzb = fm_proj([sv[:, k, 0, :] for k in range(8)], hrhs, geoE)
                cv = tmp[:, 2, 0:E]
                act_op(cv, prodf, AF.Identity, scale=V(l, V_CONVA + 16 + c))
                stt(cv[:, 2:E], prodf[:, 1:E - 1], V(l, V_CONVA + 8 + c), cv[:, 2:E], ALU.mult, ALU.add)
                stt(cv[:, 2:E], prodf[:, 0:E - 2], V(l, V_CONVA + 0 + c), cv[:, 2:E], ALU.mult, ALU.add)
                stt(v3(preT[:, c, 0:E], nse, sle), Pzb, V(l, V_BIN + c), v3(cv, nse, sle), ALU.add, ALU.mult)
                tick()
            gated_out(w_a_out[l], 6144, "set")

            zvslots = [wgrp_k8(w_in[l], 4096 + hf * 512, 512).rearrange("p (k n) -> p k n", k=8) for hf in range(2)]
            def v_s1(b):
                gv = tmp[:, 1 + b % 2, 0:1024]
                st = stat[:, (b % 2) * 16:(b % 2) * 16 + 16]
                for hf in range(2):
                    Pt = small_ps()
                    for k in range(8):
                        mm(Pt[:], hT[:, k, tcol0 + b * 128:tcol0 + (b + 1) * 128], zvslots[hf][:, k, :], k == 0, k == 7, k == 7)
                    gsl = gv[:, hf * 512:(hf + 1) * 512]
                    tt(gsl, Pt[:], bc[:, 0, hf * 512:(hf + 1) * 512], ALU.add)
                    act_op(gsl, gsl, AF.Gelu_apprx_tanh)
                for hf in range(2):
                    gsl = gv[:, hf * 512:(hf + 1) * 512]
                    R.op("dve", lambda e, hf=hf, gsl=gsl, st=st: e.bn_stats(out=st[:, hf * 6:(hf + 1) * 6], in_=gsl),
                         reads=(gsl,), writes=(st[:, hf * 6:(hf + 1) * 6],))

            def v_s2(b):
                gv = tmp[:, 1 + b % 2, 0:1024]
                st = stat[:, (b % 2) * 16:(b % 2) * 16 + 16]
                R.op("dve", lambda e: e.bn_aggr(out=st[:, 12:14], in_=st[:, 0:12].rearrange("p (a b) -> p a b", a=2)),
                     reads=(st[:, 0:12],), writes=(st[:, 12:14],))
                act_op(st[:, 14:15], st[:, 13:14], AF.Sqrt, bias=epsc[:, 0:1])
                R.op("dve", lambda e: e.reciprocal(out=st[:, 14:15], in_=st[:, 14:15]),
                     reads=(st[:, 14:15],), writes=(st[:, 14:15],))
                ts(tok[:, b, :], gv, st[:, 12:13], ALU.subtract, st[:, 14:15], ALU.mult)

            v_s1(0)
            for b in range(1, nb):
                v_s1(b)
                v_s2(b - 1)
            v_s2(nb - 1)
            for g2 in range(2):
                uslot, ui = pinned_k8(w_in[l], 3072 + g2 * 512, 512)
                for jj in range(4):
                    g = g2 * 4 + jj
                    Pu = fm_proj([uslot[:, k, jj * 128:(jj + 1) * 128] for k in range(8)], hrhs, geoE)
                    u = ubuf[:, g % 2, 0:E]
                    act_op(v3(u, nse, sle), Pu, AF.Gelu_apprx_tanh, bias=V(l, V_BIN + 24 + g))
                    Pm = big_ps()
                    Pmf = Pm[:].rearrange("p a b -> p (a b)")
                    for b in range(nb):
                        mm(Pmf[:, b * 128:(b + 1) * 128], tok[:, b, g * 128:(g + 1) * 128], wsT[:, l, g, :],
                           True, True, b == nb - 1)
                    mx = tmp[:, 0, 0:nb * 128]
                    stt(mx.rearrange("p (b i) -> p b i", b=nb), Pmf[:, 0:nb * 128].rearrange("p (b i) -> p b i", b=nb),
                        V(l, V_LNVG + g), bsb[:, g:g + 1, :].broadcast_to([128, nb, 128]), ALU.mult, ALU.add)
                    mset(preT[:, g, 0:lead], 0.0)
                    tt(preT[:, g, lead:E], mx[:, off:off + T], u[:, lead:E], ALU.mult)
                    tick()
                unpin(ui)
            gated_out(w_b_out[l], 7168, "add")

            zpslots = [wgrp_k8(w_in[l], 5120 + hf * 512, 512).rearrange("p (k n) -> p k n", k=8) for hf in range(2)]
            for b in range(nb):
                for hf in range(2):
                    Pt = small_ps()
                    for k in range(8):
                        mm(Pt[:], hT[:, k, tcol0 + b * 128:tcol0 + (b + 1) * 128], zpslots[hf][:, k, :], k == 0, k == 7, k == 7)
                    tt(tok[:, b, hf * 512:(hf + 1) * 512], Pt[:], bc[:, 1, hf * 512:(hf + 1) * 512], ALU.add)
            for c in range(8):
                kw = c // 2
                Pm = big_ps()
                Pmf = Pm[:].rearrange("p a b -> p (a b)")
                for b in range(nb):
                    band = bandf if (gtile0 + tb0 + b) == 2 else bandg
                    prev = pcar[:, l, c * 128:(c + 1) * 128] if b == 0 else tok[:, b - 1, c * 128:(c + 1) * 128]
                    o = Pmf[:, b * 128:(b + 1) * 128]
                    mm(o, tok[:, b, c * 128:(c + 1) * 128], band[:, kw, 0, :], True, False, False)
                    mm(o, prev, band[:, kw, 1, :], False, True, b == nb - 1)
                act_op(preT[:, c, lead:E], Pmf[:, off:off + T], AF.Identity)
            cp(pcar[:, l, :], tok[:, nb - 1, :])
            pslot_w = wload([
                (lambda s: s[:, 0:2048].rearrange("p (wk n) -> p wk n", wk=8),
                 w_pool[l].rearrange("w (kc p) n -> p (w kc) n", p=128)),
            ])[:, 0:2048].rearrange("p (w k n) -> p w k n", w=4, k=2)
            for g2 in range(2):
                gslot = wgrp_k8(w_in[l], 8192 + g2 * 512, 512).rearrange("p (k n) -> p k n", k=8)
                for jj in range(4):
                    j = g2 * 4 + jj
                    kw = j // 2
                    Pg = fm_proj([gslot[:, k, jj * 128:(jj + 1) * 128] for k in range(8)], hrhs, geoE)
                    gate = v3(tmp[:, 0, 0:E], nse, sle)
                    act_op(gate, Pg, AF.Sigmoid, bias=V(l, V_BIN + 64 + j))
                    Py = fm_proj([pslot_w[:, kw, kc, (j % 2) * 128:(j % 2 + 1) * 128] for kc in range(2)],
                                 lambda k, lo, hi, kw=kw: preT[:, 2 * kw + k, lo:hi], geoP)
                    t1 = v3(tmp[:, 1, 0:E], nse, sle)
                    stt(t1, Py, V(l, V_PSC + j), gate, ALU.mult, ALU.mult)
                    tt(mbf[:, j, 0:T], mrg[:, j, lead:E], tmp[:, 1, lead:E], ALU.add)

            drain()
            for g2 in range(2):
                wslot = wgrp_k8(w_o[l], g2 * 512, 512).rearrange("p (k n) -> p k n", k=8)
                for jj in range(4):
                    j = g2 * 4 + jj
                    Po = fm_proj([wslot[:, k, jj * 128:(jj + 1) * 128] for k in range(8)],
                                 lambda k, lo, hi: mbf[:, k, lo:hi], (0, ns, sl))
                    xv = v3(xT[:, j, c0:c0 + T], ns, sl)
                    stt(xv, Po, ada[:, l, 16 + j:17 + j], xv, ALU.mult, ALU.add)
                    ln_acc(j, c0, T)
            layer_norm_fm(c0, T, [
                (lambda c: xT[:, c, c0:c0 + T], lambda c: der[:, l, 4, c:c + 1], lambda c: der[:, l, 5, c:c + 1]),
                (lambda c: hT[:, c, hcol0:hcol0 + T], lambda c: der[:, l, 2, c:c + 1], lambda c: der[:, l, 3, c:c + 1]),
            ])

        def ffn(l, c0, T, gtile0, first_tile):
            lead, E, nse, sle, ns, sl, hcol0 = geom(c0, T)
            geoE = (hcol0 - lead, nse, sle)
            cp(hT[:, :, 0:LEAD], hcar[:, l, 1, :, :])
            cp(hcar[:, l, 1, :, :], hT[:, :, LEAD + TMAX - LEAD:LEAD + TMAX])
            nmask = (lead + max(0, 256 - (gtile0 * 128 + c0))) if first_tile else 0
            wup_v = w_up[l].rearrange("(kc p) (r n) -> p kc r n", p=128, r=2)

            def hrhs(k, lo, hi):
                return hT[:, k, lo:hi]

            for i in range(NFC // 2):
                slot = wload([
                    (lambda s, r=r: s[:, 0:4096].rearrange("p (k r n) -> p k r n", k=8, r=2)[:, :, r, :],
                     wup_v[:, :, r, i * 256:(i + 1) * 256]) for r in range(2)
                ])
                sv = slot[:, 0:4096].rearrange("p (k r n) -> p k r n", k=8, r=2)
                for jj in range(2):
                    c = 2 * i + jj
                    Pa = fm_proj([sv[:, k, 0, jj * 128:(jj + 1) * 128] for k in range(8)], hrhs, geoE)
                    af = tmp[:, 0, 0:E]
                    act_op(v3(af, nse, sle), Pa, AF.Identity, bias=V(l, V_BUP + c))
                    if nmask:
                        ts(af[:, 0:nmask], af[:, 0:nmask], mask[:, 0:1], ALU.mult)
                    cv = tmp[:, 1, 0:E]
                    act_op(cv, af, AF.Identity, scale=V(l, V_CONVF + 44 + c))
                    stt(cv[:, 2:E], af[:, 1:E - 1], V(l, V_CONVF + 22 + c), cv[:, 2:E], ALU.mult, ALU.add)
                    stt(cv[:, 2:E], af[:, 0:E - 2], V(l, V_CONVF + 0 + c), cv[:, 2:E], ALU.mult, ALU.add)
                    gl = tmp[:, 2, 0:E]
                    act_op(gl[:, 2:E], cv[:, 2:E], AF.Gelu_apprx_tanh, bias=V(l, V_CONVFB + c))
                    Pg = fm_proj([sv[:, k, 1, jj * 128:(jj + 1) * 128] for k in range(8)], hrhs, geoE)
                    stt(v3(fT[:, c, 0:E], nse, sle), Pg, V(l, V_BUP + NFC + c), v3(gl, nse, sle), ALU.add, ALU.mult)
            wd_v = w_down[l].rearrange("(kc p) n -> p kc n", p=128)
            for jp in range(4):
                Ps = [big_ps(), big_ps()]
                for kh in range(2):
                    slot = wload([
                        (lambda s: s[:, 0:2816].rearrange("p (k n) -> p k n", k=11),
                         wd_v[:, kh * 11:(kh + 1) * 11, jp * 256:(jp + 1) * 256]),
                    ])
                    sv = slot[:, 0:2816].rearrange("p (k n) -> p k n", k=11)
                    for jj in range(2):
                        for s in range(ns):
                            o = Ps[jj][:, s, 0:sl]
                            for k in range(11):
                                mm(o, sv[:, k, jj * 128:(jj + 1) * 128],
                                   fT[:, kh * 11 + k, lead + s * sl:lead + (s + 1) * sl],
                                   kh == 0 and k == 0, kh == 1 and k == 10, s == ns - 1 and k == 10)
                for jj in range(2):
                    j = jp * 2 + jj
                    xv = v3(xT[:, j, c0:c0 + T], ns, sl)
                    stt(xv, Ps[jj][:, 0:ns, 0:sl], ada[:, l, 40 + j:41 + j], xv, ALU.mult, ALU.add)
                    ln_acc(j, c0, T)
            if l < DEPTH - 1:
                layer_norm_fm(c0, T, [
                    (lambda c: xT[:, c, c0:c0 + T], lambda c: der[:, l, 6, c:c + 1], lambda c: der[:, l, 7, c:c + 1]),
                    (lambda c: hT[:, c, hcol0:hcol0 + T], lambda c: der[:, l + 1, 0, c:c + 1], lambda c: der[:, l + 1, 1, c:c + 1]),
                ])
            else:
                layer_norm_fm(c0, T, [
                    (lambda c: xT[:, c, c0:c0 + T], lambda c: V(l, V_LN2G + c), lambda c: V(l, V_LN2B + c)),
                ])

        iost = {"i": 0, "o": 0}

        def load_x(gblk_list):
            for bi, gb in enumerate(gblk_list):
                s = iost["i"] % 2
                iost["i"] += 1
                dma("sp", xin[:, s, :], xs[gb * 128:(gb + 1) * 128, :], "xin%d" % s, writes=(xin[:, s, :],))
                for half, P in enumerate((PC, PD)):
                    for cc in range(4):
                        c = half * 4 + cc
                        o = P[:, cc * 128:(cc + 1) * 128]
                        i_ap = xin[:, s, c * 128:(c + 1) * 128]
                        R.op("pe", lambda e, o=o, i_ap=i_ap: e.transpose(o, i_ap, ident[:]),
                             reads=(i_ap, ident[:]), writes=(o,), signal=(cc == 3))
                    ts(xT[:, half * 4:half * 4 + 4, bi * 128:(bi + 1) * 128],
                       P[:, 0:512].rearrange("p (c t) -> p c t", c=4), ALPHA, ALU.mult)

        def make_h0(T):
            for c in range(8):
                act_op(hT[:, c, LEAD:LEAD + T], xT[:, c, 0:T], AF.Identity,
                       bias=der[:, 0, 1, c:c + 1], scale=der[:, 0, 0, c:c + 1])

        def store_y(col_blocks):
            for b, ob in col_blocks:
                s = iost["o"] % 2
                iost["o"] += 1
                for half, P in enumerate((PC, PD)):
                    for cc in range(4):
                        c = half * 4 + cc
                        o = P[:, cc * 128:(cc + 1) * 128]
                        i_ap = xT[:, c, b * 128:(b + 1) * 128]
                        R.op("pe", lambda e, o=o, i_ap=i_ap: e.transpose(o, i_ap, ident[:]),
                             reads=(i_ap, ident[:]), writes=(o,), signal=(cc == 3))
                    if half == 0:
                        act_op(ost[:, s, 0:512], P[:, 0:512], AF.Identity)
                    else:
                        cp(ost[:, s, 512:1024], P[:, 0:512])
                dma("sp", y[ob * 128:(ob + 1) * 128, :], ost[:, s, :], "ost%d" % s, reads=(ost[:, s, :],))

        load_x(list(range(0, 9)))
        for g in range(4):
            ada_step(0, g)
        der_part1(0)
        make_h0(1152)
        for g in range(4, 12):
            todo.append(lambda g=g: ada_step(0, g))
        todo.append(lambda: der_part2(0))
        for g in range(12):
            todo.append(lambda g=g: ada_step(1, g))
        todo.append(lambda: der_part1(1))
        todo.append(lambda: der_part2(1))
        mixing(0, 120, 1032, 0, True)
        ffn(0, 120, 1032, 0, True)
        mixing(1, 248, 904, 0, True)
        ffn(1, 248, 904, 0, True)
        store_y([(b, b - 2) for b in range(2, 9)])
        load_x(list(range(9, 18)))
        make_h0(1152)
        mixing(0, 0, 1152, 9, False)
        ffn(0, 0, 1152, 9, False)
        mixing(1, 0, 1152, 9, False)
        ffn(1, 0, 1152, 9, False)
        store_y([(b, 7 + b) for b in range(0, 9)])
        R.wait_all("sp", [("ost0", R.val["ost0"]), ("ost1", R.val["ost1"])])

        with nc.Block() as block:
            @block.tensor
            def _(e):
                R.replay(e, "pe")

            @block.scalar
            def _(e):
                R.replay(e, "act")

            @block.vector
            def _(e):
                R.replay(e, "dve")

            @block.gpsimd
            def _(e):
                R.replay(e, "pool")

            @block.sync
            def _(e):
                R.replay(e, "sp")
    return nc


def _fm(v, nch):
    return np.ascontiguousarray(np.asarray(v, np.float32).reshape(nch, 128).T)


def _bands(first):
    out = np.zeros((128, 4, 2, 128), np.float32)
    for k, w in enumerate(POOL_W):
        for t in range(128):
            if first:
                n = min(t + 1, w)
                for tp in range(max(0, t - w + 1), t + 1):
                    out[tp, k, 0, t] += 1.0 / n
            else:
                for tp in range(t - w + 1, t + 1):
                    if tp >= 0:
                        out[tp, k, 0, t] += 1.0 / w
                    else:
                        out[tp + 128, k, 1, t] += 1.0 / w
            out[t, k, 0, t] -= 1.0
    return out.reshape(128, 4 * 2 * 128)


_CACHE = {}


def kernel(x, c, w_ada, b_ada, w_in, b_in, conv_a, w_a_out, ln_v_g, ln_v_b,
           w_spatial, b_spatial, w_b_out, w_pool, pool_scale, w_o, ln1_g, ln1_b,
           w_up, b_up, conv_ffn, conv_ffn_b, w_down, ln2_g, ln2_b):
    f = lambda a: np.ascontiguousarray(np.asarray(a, dtype=np.float32))
    x = f(x)
    c = f(c)
    vecs = np.zeros((128, DEPTH, NV), np.float32)
    rows = np.zeros((DEPTH, 128, 2, D), np.float32)
    bsb = np.zeros((DEPTH, 128, 8, 128), np.float32)
    wsT = np.zeros((128, DEPTH, 8, 128), np.float32)
    for l in range(DEPTH):
        vecs[:, l, V_BIN:V_BIN + 72] = _fm(b_in[l], 72)
        vecs[:, l, V_BADA:V_BADA + 48] = _fm(b_ada[l], 48)
        vecs[:, l, V_CONVA:V_CONVA + 24] = np.concatenate([_fm(conv_a[l][k], 8) for k in range(3)], axis=1)
        vecs[:, l, V_LN1G:V_LN1G + 8] = _fm(ln1_g[l], 8)
        vecs[:, l, V_LN1B:V_LN1B + 8] = _fm(ln1_b[l], 8)
        vecs[:, l, V_LN2G:V_LN2G + 8] = _fm(ln2_g[l], 8)
        vecs[:, l, V_LN2B:V_LN2B + 8] = _fm(ln2_b[l], 8)
        vecs[:, l, V_PSC:V_PSC + 8] = _fm(pool_scale[l], 8)
        vecs[:, l, V_BUP:V_BUP + 44] = _fm(b_up[l], 44)
        vecs[:, l, V_CONVF:V_CONVF + 66] = np.concatenate([_fm(conv_ffn[l][k], NFC) for k in range(3)], axis=1)
        vecs[:, l, V_CONVFB:V_CONVFB + NFC] = _fm(conv_ffn_b[l], NFC)
        vecs[:, l, V_LNVG:V_LNVG + 8] = _fm(ln_v_g[l], 8)
        vecs[:, l, V_LNVB:V_LNVB + 8] = _fm(ln_v_b[l], 8)
        bl = np.asarray(b_in[l], np.float32)
        rows[l, :, 0, :] = bl[4096:5120][None, :]
        rows[l, :, 1, :] = bl[5120:6144][None, :]
        bsb[l] = np.asarray(b_spatial[l], np.float32)[None, :, :]
        wsT[:, l] = np.transpose(np.asarray(w_spatial[l], np.float32), (2, 0, 1))
    wsT = np.ascontiguousarray(wsT.reshape(128, DEPTH * 8 * 128))
    bandg = _bands(False)
    bandf = _bands(True)
    ident = np.eye(128, dtype=np.float32)
    shared = {
        "vecs": vecs, "rows": rows, "bsb": bsb, "wsT": wsT, "bandg": bandg, "ident": ident,
        "w_ada": f(w_ada), "w_in": f(w_in), "w_a_out": f(w_a_out), "w_b_out": f(w_b_out),
        "w_pool": f(w_pool), "w_o": f(w_o), "w_up": f(w_up), "w_down": f(w_down),
    }
    in_maps = []
    for core in range(8):
        b, half = core // 2, core % 2
        xs = np.zeros((TOK_IN, D), np.float32)
        if half == 0:
            xs[256:] = x[b, 0:2048]
        else:
            xs[:] = x[b, 2048 - 256:4096]
        m = dict(shared)
        m["xs"] = xs
        m["cvec"] = _fm(c[b], 8)
        m["mask"] = np.full((128, 1), float(half), np.float32)
        m["bandf"] = bandf if half == 0 else bandg
        in_maps.append(m)
    if "nc" not in _CACHE:
        _CACHE["nc"] = build_program()
    res = run_bass_kernel_spmd(_CACHE["nc"], in_maps, core_ids=list(range(8)))
    out = np.zeros((NBATCH, SEQ, D), np.float32)
    for core in range(8):
        b, half = core // 2, core % 2
        out[b, half * 2048:(half + 1) * 2048] = res.results[core]["y"]
    return out
```
